# Optimizing a Trainium2 kernel written in Bass

```python
import jax, jax.numpy as jnp
from jax import lax
import numpy as np

D_MODEL = 1024
BATCH = 8
SEQ = 8192
DEPTH = 1

D_PLE = 256
D_MIX = 2 * D_MODEL
GM_WIDTH = D_MIX // 2
GM_HEADS = 8
GM_HEAD_DIM = GM_WIDTH // GM_HEADS
GM_CHUNK = 128
SSM_WIDTH = D_MIX - GM_WIDTH
SSM_HEAD_DIM = 64
SSM_HEADS = SSM_WIDTH // SSM_HEAD_DIM
SSM_GROUPS = 2
SSM_STATE = 128
SSM_CONV = 4
SSM_CHUNK = 128
SSM_CONV_DIM = SSM_WIDTH + 2 * SSM_GROUPS * SSM_STATE
D_FF = 256 * ((8 * D_MODEL // 3 + 255) // 256)
EPS = 1e-6
IN_SPLITS = (GM_WIDTH, 2 * GM_WIDTH, 2 * GM_WIDTH + SSM_WIDTH, 2 * GM_WIDTH + SSM_WIDTH + SSM_CONV_DIM)
IN_PROJ_DIM = 2 * GM_WIDTH + SSM_WIDTH + SSM_CONV_DIM + SSM_HEADS

kernel_name = "hybrid_gmlp_ssd_macaron_block"


def rmsnorm(x, g):
    xf = x.astype(jnp.float32)
    y = xf * lax.rsqrt(jnp.mean(xf * xf, axis=-1, keepdims=True) + EPS)
    return (y * g.astype(jnp.float32)).astype(x.dtype)


def layernorm(x, g, b):
    xf = x.astype(jnp.float32)
    mu = jnp.mean(xf, axis=-1, keepdims=True)
    xc = xf - mu
    y = xc * lax.rsqrt(jnp.mean(xc * xc, axis=-1, keepdims=True) + EPS)
    return (y * g.astype(jnp.float32) + b.astype(jnp.float32)).astype(x.dtype)


def swiglu(x, w_gate, w_up, w_down):
    return (jax.nn.silu(x @ w_gate) * (x @ w_up)) @ w_down


def chunked_spatial_gating(u, v, ln_g, ln_b, w_s, b_s):
    bsz, L, _ = u.shape
    nc = L // GM_CHUNK
    v = layernorm(v, ln_g, ln_b).reshape(bsz, nc, GM_CHUNK, GM_HEADS, GM_HEAD_DIM)
    mask = jnp.tril(jnp.ones((GM_CHUNK, GM_CHUNK), dtype=bool))
    w = jnp.where(mask, w_s, jnp.zeros_like(w_s)).astype(v.dtype)
    mixed = jnp.einsum("hts,bcshd->bcthd", w, v) + b_s.T.astype(v.dtype)[None, None, :, :, None]
    return u * mixed.reshape(bsz, L, GM_WIDTH)


def causal_depthwise_conv(x, w, b):
    y = lax.conv_general_dilated(
        x, w[:, None, :].astype(x.dtype), window_strides=(1,), padding=[(SSM_CONV - 1, 0)],
        dimension_numbers=("NWC", "WIO", "NWC"), feature_group_count=x.shape[-1])
    return y + b.astype(x.dtype)


def segsum_exp(cs):
    T = cs.shape[-1]
    diff = cs[..., :, None] - cs[..., None, :]
    mask = jnp.tril(jnp.ones((T, T), dtype=bool))
    return jnp.exp(jnp.where(mask, diff, -jnp.inf))


def ssd_chunked(x, dt, a, bm, cm):
    bsz, L, H, P = x.shape
    nc = L // SSM_CHUNK
    k = H // SSM_GROUPS
    xdt = (x * dt[..., None]).reshape(bsz, nc, SSM_CHUNK, SSM_GROUPS, k, P)
    adt = (dt * a).reshape(bsz, nc, SSM_CHUNK, SSM_GROUPS, k).transpose(0, 3, 4, 1, 2)
    bm = bm.reshape(bsz, nc, SSM_CHUNK, SSM_GROUPS, SSM_STATE)
    cm = cm.reshape(bsz, nc, SSM_CHUNK, SSM_GROUPS, SSM_STATE)
    a_cs = jnp.cumsum(adt, axis=-1)
    decay = segsum_exp(a_cs)
    cb = jnp.einsum("bclgn,bcsgn->bgcls", cm, bm)
    y_diag = jnp.einsum("bgkcls,bcsgkp->bclgkp", cb[:, :, None] * decay, xdt)
    decay_states = jnp.exp(a_cs[..., -1:] - a_cs).transpose(0, 3, 4, 1, 2)
    states = jnp.einsum("bclgn,bclgkp->bcgkpn", bm, xdt * decay_states[..., None])
    chunk_tot = jnp.pad(a_cs[..., -1], ((0, 0), (0, 0), (0, 0), (1, 0)))
    decay_chunk = segsum_exp(jnp.cumsum(chunk_tot, axis=-1))
    states = jnp.concatenate([jnp.zeros_like(states[:, :1]), states], axis=1)
    new_states = jnp.einsum("bgkzc,bcgkpn->bzgkpn", decay_chunk, states)
    prev_states = new_states[:, :-1]
    out_decay = jnp.exp(a_cs).transpose(0, 3, 4, 1, 2)
    y_off = jnp.einsum("bclgn,bcgkpn->bclgkp", cm, prev_states) * out_decay[..., None]
    return (y_diag + y_off).reshape(bsz, L, H, P)


def mamba2_mixer(z, xbc, dt_raw, conv_w, conv_b, dt_bias, a_log, d_skip, norm_g):
    bsz, L, _ = z.shape
    f32 = jnp.float32
    xbc = jax.nn.silu(causal_depthwise_conv(xbc, conv_w, conv_b))
    xs, bm, cm = jnp.split(xbc, [SSM_WIDTH, SSM_WIDTH + SSM_GROUPS * SSM_STATE], axis=-1)
    xs = xs.reshape(bsz, L, SSM_HEADS, SSM_HEAD_DIM).astype(f32)
    bm = bm.reshape(bsz, L, SSM_GROUPS, SSM_STATE).astype(f32)
    cm = cm.reshape(bsz, L, SSM_GROUPS, SSM_STATE).astype(f32)
    dt = jax.nn.softplus(dt_raw.astype(f32) + dt_bias.astype(f32))
    a = -jnp.exp(a_log.astype(f32))
    y = ssd_chunked(xs, dt, a, bm, cm) + xs * d_skip.astype(f32)[:, None]
    y = y.reshape(bsz, L, SSM_WIDTH) * jax.nn.silu(z.astype(f32))
    y = y.reshape(bsz, L, SSM_GROUPS, SSM_WIDTH // SSM_GROUPS)
    y = y * lax.rsqrt(jnp.mean(y * y, axis=-1, keepdims=True) + EPS)
    return (y.reshape(bsz, L, SSM_WIDTH) * norm_g.astype(f32)).astype(z.dtype)


def setup_inputs(seed: int = 0) -> dict:
    key = jax.random.key(seed)
    ks = iter(jax.random.split(key, 40))

    def nrm(shape, scale):
        return jax.random.normal(next(ks), shape, jnp.float32) * scale

    def gain(shape):
        return 1.0 + 0.1 * jax.random.normal(next(ks), shape, jnp.float32)

    L = DEPTH
    x = jax.random.normal(next(ks), (BATCH, SEQ, D_MODEL), jnp.float32)
    p = jax.random.normal(next(ks), (DEPTH, BATCH, SEQ, D_PLE), jnp.float32)
    dt0 = jnp.exp(jax.random.uniform(next(ks), (L, SSM_HEADS), jnp.float32)
                  * (np.log(0.1) - np.log(0.001)) + np.log(0.001))
    dt0 = jnp.maximum(dt0, 1e-4)
    dt_bias = dt0 + jnp.log(-jnp.expm1(-dt0))
    a_log = jnp.log(jax.random.uniform(next(ks), (L, SSM_HEADS), jnp.float32, 1.0, 16.0))
    return {
        "x": x,
        "p": p,
        "ffn1_norm": gain((L, D_MODEL)),
        "ffn1_w_gate": nrm((L, D_MODEL, D_FF), D_MODEL ** -0.5),
        "ffn1_w_up": nrm((L, D_MODEL, D_FF), D_MODEL ** -0.5),
        "ffn1_w_down": nrm((L, D_FF, D_MODEL), D_FF ** -0.5),
        "mix_norm": gain((L, D_MODEL)),
        "w_in": nrm((L, D_MODEL, IN_PROJ_DIM), D_MODEL ** -0.5),
        "gm_ln_g": gain((L, GM_WIDTH)),
        "gm_ln_b": nrm((L, GM_WIDTH), 0.02),
        "gm_w_s": nrm((L, GM_HEADS, GM_CHUNK, GM_CHUNK), 0.5 * GM_CHUNK ** -0.5),
        "gm_b_s": gain((L, GM_HEADS, GM_CHUNK)),
        "gm_out_norm": gain((L, GM_WIDTH)),
        "conv_w": nrm((L, SSM_CONV, SSM_CONV_DIM), SSM_CONV ** -0.5),
        "conv_b": nrm((L, SSM_CONV_DIM), 0.02),
        "dt_bias": dt_bias,
        "a_log": a_log,
        "d_skip": gain((L, SSM_HEADS)),
        "ssm_norm": gain((L, SSM_WIDTH)),
        "w_out": nrm((L, D_MIX, D_MODEL), D_MIX ** -0.5),
        "ffn2_norm": gain((L, D_MODEL)),
        "ffn2_w_gate": nrm((L, D_MODEL, D_FF), D_MODEL ** -0.5),
        "ffn2_w_up": nrm((L, D_MODEL, D_FF), D_MODEL ** -0.5),
        "ffn2_w_down": nrm((L, D_FF, D_MODEL), D_FF ** -0.5),
        "ple_norm": gain((L, D_MODEL)),
        "ple_w_gate": nrm((L, D_MODEL, D_MODEL), D_MODEL ** -0.5),
        "ple_b_gate": nrm((L, D_MODEL), 0.02),
        "ple_w_proj": nrm((L, D_PLE, D_MODEL), D_PLE ** -0.5),
        "final_norm": gain((D_MODEL,)),
    }


def reference(x, p, ffn1_norm, ffn1_w_gate, ffn1_w_up, ffn1_w_down, mix_norm, w_in,
              gm_ln_g, gm_ln_b, gm_w_s, gm_b_s, gm_out_norm, conv_w, conv_b, dt_bias, a_log,
              d_skip, ssm_norm, w_out, ffn2_norm, ffn2_w_gate, ffn2_w_up, ffn2_w_down,
              ple_norm, ple_w_gate, ple_b_gate, ple_w_proj, final_norm):
    h = x
    for i in range(DEPTH):
        h = h + 0.5 * swiglu(rmsnorm(h, ffn1_norm[i]), ffn1_w_gate[i], ffn1_w_up[i], ffn1_w_down[i])
        n = rmsnorm(h, mix_norm[i])
        proj = n @ w_in[i]
        u, v, z, xbc, dt_raw = jnp.split(proj, IN_SPLITS, axis=-1)
        ya = chunked_spatial_gating(jax.nn.gelu(u, approximate=False), jax.nn.gelu(v, approximate=False),
                                    gm_ln_g[i], gm_ln_b[i], gm_w_s[i], gm_b_s[i])
        ya = rmsnorm(ya, gm_out_norm[i])
        yb = mamba2_mixer(z, xbc, dt_raw, conv_w[i], conv_b[i], dt_bias[i], a_log[i], d_skip[i], ssm_norm[i])
        h = h + jnp.concatenate([ya, yb], axis=-1) @ w_out[i]
        h = h + 0.5 * swiglu(rmsnorm(h, ffn2_norm[i]), ffn2_w_gate[i], ffn2_w_up[i], ffn2_w_down[i])
        gate = jax.nn.sigmoid(rmsnorm(h, ple_norm[i]) @ ple_w_gate[i] + ple_b_gate[i])
        h = h + gate * (p[i] @ ple_w_proj[i])
    return rmsnorm(h, final_norm)
```

```python
import numpy as np
from contextlib import ExitStack
import concourse.bass as bass
import concourse.mybir as mybir
from concourse.bass_utils import run_bass_kernel_spmd

F32 = mybir.dt.float32
BF16 = mybir.dt.bfloat16
AF = mybir.ActivationFunctionType
ALU = mybir.AluOpType
AX = mybir.AxisListType

D = 1024
KD = 8
DFF = 2816
KF = 22
T = 512
NCHK = 4
SEQ = 8192
NH = 16
HP = 64
DIN = 4624
EPS = 1e-6
N_CORES = 8


class V:
    __slots__ = ("ap", "k")

    def __init__(self, ap, k):
        self.ap = ap
        self.k = tuple(k)

    def __getitem__(self, idx):
        return V(self.ap[idx], self.k)

    def re(self, pat, **kw):
        return V(self.ap.rearrange(pat, **kw), self.k)

    def bcast(self, shape):
        return V(self.ap.broadcast_to(shape), self.k)

    def bitcast(self, dt):
        return V(self.ap.bitcast(dt), self.k)

    def keys(self, k):
        return V(self.ap, k)


class Tl:
    def __init__(self, handle, key):
        self.t = handle
        self.key = key

    def __getitem__(self, idx):
        return V(self.t[idx], (self.key,))

    def v(self, idx, keys):
        return V(self.t[idx], keys)


ENGS = ("pe", "act", "dve", "pool", "sp")


class Sched:
    def __init__(self):
        self.ops = {e: [] for e in ENGS}
        self.last_w = {}
        self.readers = {}
        self.dma_cnt = {}
        self.dma_group = set()

    def _deps(self, eng, reads, writes):
        deps = set()
        for b in reads:
            if b in self.last_w:
                d = self.last_w[b]
                deps.add(d)
        for b in writes:
            if b in self.last_w:
                d = self.last_w[b]
                deps.add(d)
            for d in self.readers.get(b, ()):
                deps.add(d)
        out = []
        for d in deps:
            if d[0] == "dma":
                out.append(d)
            else:
                e2, i2 = d
                if e2 == eng and eng in ("pe", "sp"):
                    continue
                self.ops[e2][i2]["sig"] = True
                out.append(d)
        return out

    def op(self, eng, emit, outs=(), ins=()):
        reads = [k for v in ins for k in v.k]
        writes = [k for v in outs for k in v.k]
        deps = self._deps(eng, reads, writes)
        idx = len(self.ops[eng])
        self.ops[eng].append(dict(emit=emit, deps=deps, sig=False, dma=None))
        tok = (eng, idx)
        for b in writes:
            self.last_w[b] = tok
            self.readers[b] = []
        for b in reads:
            if b not in writes:
                self.readers.setdefault(b, []).append(tok)
        return tok

    def dma(self, queue, semkey, out, in_, group=False, **kw):
        reads = list(in_.k)
        writes = list(out.k)
        deps = self._deps(queue, reads, writes)
        if group:
            deps = [d for d in deps if not (d[0] == "dma" and d[1] == semkey)]
        cnt = self.dma_cnt.get(semkey, 0) + 1
        self.dma_cnt[semkey] = cnt
        if group:
            self.dma_group.add(semkey)
        o_ap, i_ap = out.ap, in_.ap
        idx = len(self.ops[queue])
        self.ops[queue].append(dict(emit=lambda e: e.dma_start(out=o_ap, in_=i_ap, **kw), deps=deps, sig=False, dma=semkey))
        tok = ("dma", semkey, cnt)
        for b in writes:
            self.last_w[b] = tok
            self.readers[b] = []
        for b in reads:
            self.readers.setdefault(b, []).append(tok)
        return tok

    def final_wait_all(self, eng):
        deps = []
        for semkey, cnt in self.dma_cnt.items():
            deps.append(("dma", semkey, cnt))
        self.ops[eng].append(dict(emit=None, deps=deps, sig=False, dma=None))

    def emit_all(self, nc, block, sems, dma_sems):
        signum = {}
        for e in ENGS:
            n = 0
            m = {}
            for i, o in enumerate(self.ops[e]):
                if o["sig"]:
                    n += 1
                    m[i] = n
            signum[e] = m

        def run(eng_name, eng):
            waited = {}
            for o in self.ops[eng_name]:
                need = {}
                for d in o["deps"]:
                    if d[0] == "dma":
                        _, semkey, cnt = d
                        if semkey in self.dma_group:
                            cnt = self.dma_cnt[semkey]
                        key = ("dma", semkey)
                        val = 16 * cnt
                        sem = dma_sems[semkey]
                    else:
                        e2, i2 = d
                        key = ("eng", e2)
                        val = signum[e2][i2]
                        sem = sems[e2]
                    if waited.get(key, 0) >= val:
                        continue
                    if key not in need or need[key][1] < val:
                        need[key] = (sem, val)
                for key, (sem, val) in need.items():
                    eng.wait_ge(sem, val)
                    waited[key] = val
                if o["emit"] is None:
                    continue
                ins = o["emit"](eng)
                if o["dma"] is not None:
                    ins.then_inc(dma_sems[o["dma"]], 16)
                elif o["sig"]:
                    ins.then_inc(sems[eng_name], 1)

        @block.tensor
        def _(e):
            run("pe", e)

        @block.scalar
        def _(e):
            run("act", e)

        @block.vector
        def _(e):
            run("dve", e)

        @block.gpsimd
        def _(e):
            run("pool", e)

        @block.sync
        def _(e):
            run("sp", e)


WNAMES = ["ffn1_norm", "ffn1_w_gate", "ffn1_w_up", "ffn1_w_down", "mix_norm", "w_in", "gm_ln_g", "gm_ln_b",
          "gm_w_s", "gm_b_s", "gm_out_norm", "conv_w", "conv_b", "dt_bias", "a_log", "d_skip", "ssm_norm",
          "w_out", "ffn2_norm", "ffn2_w_gate", "ffn2_w_up", "ffn2_w_down", "ple_norm", "ple_w_gate",
          "ple_b_gate", "ple_w_proj", "final_norm"]
WSHAPES = {
    "ffn1_norm": [1, 1024], "ffn1_w_gate": [1, 1024, 2816], "ffn1_w_up": [1, 1024, 2816], "ffn1_w_down": [1, 2816, 1024],
    "mix_norm": [1, 1024], "w_in": [1, 1024, 4624], "gm_ln_g": [1, 1024], "gm_ln_b": [1, 1024],
    "gm_w_s": [1, 8, 128, 128], "gm_b_s": [1, 8, 128], "gm_out_norm": [1, 1024], "conv_w": [1, 4, 1536],
    "conv_b": [1, 1536], "dt_bias": [1, 16], "a_log": [1, 16], "d_skip": [1, 16], "ssm_norm": [1, 1024],
    "w_out": [1, 2048, 1024], "ffn2_norm": [1, 1024], "ffn2_w_gate": [1, 1024, 2816], "ffn2_w_up": [1, 1024, 2816],
    "ffn2_w_down": [1, 2816, 1024], "ple_norm": [1, 1024], "ple_w_gate": [1, 1024, 1024], "ple_b_gate": [1, 1024],
    "ple_w_proj": [1, 256, 1024], "final_norm": [1024],
}


def build_nc(nt=SEQ // T, dbg=False):
    seq = nt * T
    nc = bass.Bass("TRN2", target_bir_lowering=False)
    S = Sched()
    es = ExitStack()

    def dram(name, shape, dt, kind):
        return nc.dram_tensor(name, shape, dt, kind=kind).ap()

    x_d = dram("x", [seq, D], F32, "ExternalInput")
    p_d = dram("p", [seq, 256], F32, "ExternalInput")
    out_d = dram("out", [seq, D], F32, "ExternalOutput")
    W = {n: dram(n, WSHAPES[n], F32, "ExternalInput") for n in WNAMES}

    sc = {}
    for f in (1, 2):
        sc[f"g{f}"] = dram(f"sc_g{f}", [KF, 128, KD * 128], BF16, "Internal")
        sc[f"u{f}"] = dram(f"sc_u{f}", [KF, 128, KD * 128], BF16, "Internal")
        sc[f"d{f}"] = dram(f"sc_d{f}", [KD, 128, KF * 128], BF16, "Internal")
    sc["uvz"] = dram("sc_uvz", [6, 128, KD * 512], BF16, "Internal")
    sc["xbc"] = dram("sc_xbc", [12, 128, KD * 128], BF16, "Internal")
    sc["dt"] = dram("sc_dt", [128, KD * 16], BF16, "Internal")
    sc["wo"] = dram("sc_wo", [KD, 128, 16 * 128], BF16, "Internal")
    sc["pg"] = dram("sc_pg", [KD, 128, KD * 128], BF16, "Internal")
    sc["pp"] = dram("sc_pp", [KD, 128, 2 * 128], BF16, "Internal")

    def sb(name, shape, dt, key=None):
        h = es.enter_context(nc.sbuf_tensor(name, shape, dt))
        return Tl(h, key or name)

    hT = sb("hT", [128, KD, T], F32)
    nT = sb("nT", [128, KD, T], BF16)
    G = sb("G", [128, 24, T], BF16)
    sq = [sb(f"sq{i}", [128, T], BF16) for i in range(2)]
    srt = sb("srt", [128, T], F32)
    rstd = sb("rstd", [128, T], F32)
    sgt = [sb(f"sgt{i}", [128, T], BF16) for i in range(2)]
    NSM, NBG = 8, 3
    smr = [sb(f"smr{i}", [128, 1024], BF16) for i in range(NSM)]
    bgr = [sb(f"bgr{i}", [128, 4096], BF16) for i in range(NBG)]
    wdt = sb("wdt", [128, KD * 16], BF16)
    xbcT = sb("xbcT", [128, 12, T], BF16)
    pre = [sb(f"pre{i}", [128, T + 3], F32) for i in range(2)]
    halo = sb("halo", [128, 12, 3], F32)
    cacc = [sb(f"cacc{i}", [128, T], F32) for i in range(2)]
    Sst = sb("Sst", [128, 1024], F32)
    Sbf = sb("Sbf", [128, 1024], BF16)
    io = [sb(f"io{i}", [128, 1024], F32) for i in range(2)]
    pin = [sb(f"pin{i}", [128, 256], F32) for i in range(2)]
    pT = sb("pT", [128, 2, T], BF16)
    gsb = [sb(f"gsb{i}", [128, T], F32) for i in range(2)]
    WsT = sb("WsT", [128, 8, 128], BF16)
    CstT = sb("CstT", [128, 1024], F32)
    glnb = sb("glnb", [128, 1024], F32)
    sel = sb("sel", [16, 16, 128], F32)
    identb = sb("identb", [128, 128], BF16)
    identf = sb("identf", [128, 128], F32)
    mle = sb("mle", [128, 128], F32)
    mge = sb("mge", [128, 128], F32)
    negm = sb("negm", [128, 128], BF16)
    onesf = sb("onesf", [128, 128], F32)
    onesb = sb("onesb", [128, 128], BF16)
    vecT = sb("vecT", [128, 8, 8], F32)
    cw = sb("cw", [128, 12, 4], F32)
    cb = sb("cb", [128, 12], F32)
    dtb = sb("dtb", [128, 16], F32)
    abc = sb("abc", [128, 16], F32)
    dsk = sb("dsk", [128, 16], F32)
    bsT = sb("bsT", [128, 8], F32)
    rw = sb("rw", [128, 8], F32)
    zB = sb("zB", [128, NCHK, 1024], BF16)
    v_g = sb("v_g", [128, 1024], F32)
    mx = sb("mx", [128, 1024], F32)
    yan = sb("yan", [128, 1024], BF16)
    xs_tok = sb("xs_tok", [128, 1024], BF16)
    xdt = sb("xdt", [128, 1024], BF16)
    xdtds = sb("xdtds", [128, 1024], BF16)
    Btok = sb("Btok", [128, 256], BF16)
    decT = sb("decT", [128, 16, 128], BF16)
    MT = sb("MT", [128, 16, 128], BF16)
    cbm = sb("cbm", [128, 2, 128], F32)
    y1 = sb("y1", [128, 1024], F32)
    y2 = sb("y2", [128, 1024], F32)
    ybn = sb("ybn", [128, 1024], BF16)
    junk = sb("junk", [128, 1024], BF16)
    st6 = sb("st6", [128, 2, 6], F32)
    mv = sb("mv", [128, 2], F32)
    sm = {n: sb("sm_" + n, [128, 16], F32) for n in
          ["t16a", "t16b", "dt", "adt", "nacs", "eacs", "t16c", "ds", "dtds", "etot"]}
    acsT = sb("acsT", [16, 128], F32)
    s1 = {n: sb("s1_" + n, [128, 2], F32) for n in ["lnr", "lnr2", "lnb", "ssa", "ra", "ra2", "ssb", "rb", "rb2"]}

    banks = [Tl(es.enter_context(nc.psum_tensor(f"bank{i}", [128, 512], F32)), f"bank{i}") for i in range(8)]
    bank_ctr = [0]

    def bank():
        b = banks[bank_ctr[0] % 8]
        bank_ctr[0] += 1
        return b

    def mm(out, lhsT, rhs, start=True, stop=True):
        o, l, r = out.ap, lhsT.ap, rhs.ap
        S.op("pe", lambda e: e.matmul(o, lhsT=l, rhs=r, start=start, stop=stop), outs=[out], ins=[lhsT, rhs])

    def tr(out, in_, ident):
        o, i, d = out.ap, in_.ap, ident.ap
        S.op("pe", lambda e: e.transpose(o, i, d), outs=[out], ins=[in_, ident])

    def act(out, in_, func, bias=None, scale=None, accum=None, eng="act"):
        o, i = out.ap, in_.ap
        kw = {}
        ins = [in_]
        outs = [out]
        if bias is not None:
            if isinstance(bias, V):
                kw["bias"] = bias.ap
                ins.append(bias)
            else:
                kw["bias"] = bias
        if scale is not None:
            if isinstance(scale, V):
                kw["scale"] = scale.ap
                ins.append(scale)
            else:
                kw["scale"] = scale
        if accum is not None:
            kw["accum_out"] = accum.ap
            outs.append(accum)
        S.op("act", lambda e: e.activation(out=o, in_=i, func=func, **kw), outs=outs, ins=ins)

    def tt(eng, out, in0, in1, op):
        o, a, b = out.ap, in0.ap, in1.ap
        S.op(eng, lambda e: e.tensor_tensor(out=o, in0=a, in1=b, op=op), outs=[out], ins=[in0, in1])

    def ts(eng, out, in0, s1_, s2_, op0, op1=None):
        o, a = out.ap, in0.ap
        ins = [in0]
        a1 = s1_
        a2 = s2_
        if isinstance(s1_, V):
            ins.append(s1_)
            a1 = s1_.ap
        if isinstance(s2_, V):
            ins.append(s2_)
            a2 = s2_.ap
        if op1 is None:
            S.op(eng, lambda e: e.tensor_scalar(out=o, in0=a, scalar1=a1, scalar2=None, op0=op0), outs=[out], ins=ins)
        else:
            S.op(eng, lambda e: e.tensor_scalar(out=o, in0=a, scalar1=a1, scalar2=a2, op0=op0, op1=op1), outs=[out], ins=ins)

    def stt(out, in0, scalar, in1, op0, op1):
        o, a, b = out.ap, in0.ap, in1.ap
        ins = [in0, in1]
        sc_ = scalar
        if isinstance(scalar, V):
            ins.append(scalar)
            sc_ = scalar.ap
        S.op("dve", lambda e: e.scalar_tensor_tensor(out=o, in0=a, scalar=sc_, in1=b, op0=op0, op1=op1), outs=[out], ins=ins)

    def cp(eng, out, in_):
        o, i = out.ap, in_.ap
        if eng == "act":
            S.op("act", lambda e: e.activation(out=o, in_=i, func=AF.Copy), outs=[out], ins=[in_])
        else:
            S.op(eng, lambda e: e.tensor_copy(out=o, in_=i), outs=[out], ins=[in_])

    def recip(out, in_):
        o, i = out.ap, in_.ap
        S.op("dve", lambda e: e.reciprocal(out=o, in_=i), outs=[out], ins=[in_])

    def memset(eng, out, val):
        o = out.ap
        S.op(eng, lambda e: e.memset(o, val), outs=[out])

    def dv(ap, key):
        return V(ap, (key,))

    pro = "pro"

    def pload(out, in_ap, key="w_in_dram"):
        S.dma("sp", pro, out, dv(in_ap, key), group=True, allow_slow_non_contiguous=True)

    for i, n in enumerate(["ffn1_norm", "mix_norm", "ffn2_norm", "ple_norm", "final_norm", "ple_b_gate", "gm_out_norm", "ssm_norm"]):
        src = W[n] if n == "final_norm" else W[n][0]
        pload(vecT[:, i, :], src.rearrange("(kc p) -> p kc", p=128))
    for j in range(12):
        pload(cw[:, j, :], W["conv_w"][0][:, j * 128:(j + 1) * 128].rearrange("k c -> c k"))
    pload(cb[:, :], W["conv_b"][0].rearrange("(j c) -> c j", c=128))
    pload(dtb[:, :], W["dt_bias"][0:1, :].broadcast_to([128, 16]))
    pload(abc[:, :], W["a_log"][0:1, :].broadcast_to([128, 16]))
    pload(dsk[:, :], W["d_skip"][0:1, :].broadcast_to([128, 16]))
    pload(bsT[:, :], W["gm_b_s"][0].rearrange("h t -> t h"))
    pload(glnb[:, :], W["gm_ln_g"][0:1, :].broadcast_to([128, 1024]))
    pload(y1[:, :], W["gm_ln_b"][0:1, :].broadcast_to([128, 1024]))

    memset("pool", onesf[:, :], 1.0)
    memset("pool", onesb[:, :], 1.0)
    memset("pool", halo[:, :, :], 0.0)
    memset("pool", Sst[:, :], 0.0)
    memset("pool", Sbf[:, :], 0.0)

    def asel(out, in_, cmp, base, cm, pat):
        o, i = out.ap, in_.ap
        S.op("pool", lambda e: e.affine_select(out=o, in_=i, pattern=pat, compare_op=cmp, fill=0.0, base=base,
                                               channel_multiplier=cm), outs=[out], ins=[in_])

    asel(identf[:, :], onesf[:, :], ALU.is_equal, 0, 1, [[-1, 128]])
    asel(mle[:, :], onesf[:, :], ALU.is_ge, 0, -1, [[1, 128]])
    asel(mge[:, :], onesf[:, :], ALU.is_ge, 0, 1, [[-1, 128]])
    cp("pool", identb[:, :], identf[:, :])
    memset("pool", negm[:, :], -30000.0)
    asel(negm[:, :], negm[:, :], ALU.is_gt, 0, 1, [[-1, 128]])
    memset("pool", sel[:, :, :], 1.0)
    asel(sel[:, :, :], sel[:, :, :], ALU.is_equal, 0, 1, [[-1, 16], [0, 128]])
    act(abc[:, :], abc[:, :], AF.Exp)
    ts("dve", abc[:, :], abc[:, :], -1.0, None, ALU.mult)
    for h in range(8):
        wt = io[h % 2]
        S.dma("sp", f"io{h % 2}", wt[:, 0:128], dv(W["gm_w_s"][0, h], "w_in_dram"))
        b = bank()
        tr(b[:, 0:128], wt[:, 0:128], identf[:, :])
        tt("dve", WsT[:, h, :], b[:, 0:128], mle[:, :], ALU.mult)
        tt("pool", wt[:, 128:256], wt[:, 0:128], mge[:, :], ALU.mult)
        o_, i_ = rw[:, h:h + 1], wt[:, 128:256]
        S.op("dve", (lambda o_=o_, i_=i_: lambda e: e.reduce_sum(out=o_.ap, in_=i_.ap, axis=AX.X))(), outs=[o_], ins=[i_])
    for h in range(8):
        ts("dve", CstT[:, h * 128:(h + 1) * 128], y1[:, h * 128:(h + 1) * 128], rw[:, h:h + 1], bsT[:, h:h + 1], ALU.mult, ALU.add)

    def conv_dma(grp, out_ap, in_ap, outkey):
        S.dma("pool", grp, dv(out_ap, outkey), dv(in_ap, "w_in_dram"), group=True)

    def conv_ffn(f):
        grp = f"cv_f{f}"
        wg, wu, wd = W[f"ffn{f}_w_gate"][0], W[f"ffn{f}_w_up"][0], W[f"ffn{f}_w_down"][0]
        wgv = wg.rearrange("(kc p) f -> p kc f", p=128)
        wuv = wu.rearrange("(kc p) f -> p kc f", p=128)
        for fc in range(KF):
            conv_dma(grp, sc[f"g{f}"][fc].rearrange("p (kc f) -> p kc f", kc=KD), wgv[:, :, fc * 128:(fc + 1) * 128], (f"g{f}", fc))
            conv_dma(grp, sc[f"u{f}"][fc].rearrange("p (kc f) -> p kc f", kc=KD), wuv[:, :, fc * 128:(fc + 1) * 128], (f"u{f}", fc))
        wdv = wd.rearrange("(fc p) d -> p fc d", p=128)
        for dc in range(KD):
            conv_dma(grp, sc[f"d{f}"][dc].rearrange("p (fc d) -> p fc d", fc=KF), wdv[:, :, dc * 128:(dc + 1) * 128], (f"d{f}", dc))

    conv_ffn(1)
    winv = W["w_in"][0].rearrange("(kc p) f -> p kc f", p=128)
    for j in range(12):
        conv_dma("cv_in", sc["xbc"][j].rearrange("p (kc f) -> p kc f", kc=KD), winv[:, :, 3072 + j * 128:3072 + (j + 1) * 128], ("xbc", j))
    for i in range(6):
        conv_dma("cv_in", sc["uvz"][i].rearrange("p (kc f) -> p kc f", kc=KD), winv[:, :, i * 512:(i + 1) * 512], ("uvz", i))
    conv_dma("cv_in", sc["dt"].rearrange("p (kc f) -> p kc f", kc=KD), winv[:, :, 4608:4624], ("dt", 0))
    wov = W["w_out"][0].rearrange("(kc p) d -> p kc d", p=128)
    wosc = sc["wo"].rearrange("dc p (kc d) -> p kc dc d", kc=16)
    for kc in range(16):
        wt = io[kc % 2]
        S.dma("sp", f"io{kc % 2}", wt[:, :], dv(wov[:, kc, :], "w_in_dram"))
        gain = vecT[:, 6, kc:kc + 1] if kc < 8 else vecT[:, 7, kc - 8:kc - 7]
        ts("dve", yan[:, :] if kc % 2 == 0 else ybn[:, :], wt[:, :], gain, None, ALU.mult)
        src = yan if kc % 2 == 0 else ybn
        S.dma("sp", f"cv_wo{kc % 2}", dv(wosc[:, kc], ("wo_part", kc)), src[:, :].re("p (dc d) -> p dc d", dc=KD))
    def conv_late():
        conv_ffn(2)
        wpgv = W["ple_w_gate"][0].rearrange("(kc p) f -> p kc f", p=128)
        for dc in range(KD):
            conv_dma("cv_ple", sc["pg"][dc].rearrange("p (kc f) -> p kc f", kc=KD), wpgv[:, :, dc * 128:(dc + 1) * 128], ("pg", dc))
        wppv = W["ple_w_proj"][0].rearrange("(kc p) f -> p kc f", p=128)
        for j in range(KD):
            conv_dma("cv_ple", sc["pp"][j].rearrange("p (kc f) -> p kc f", kc=2), wppv[:, :, j * 128:(j + 1) * 128], ("pp", j))

    sm_uses, bg_uses = [], []
    def seq_for_tile():
        sm_, bg_ = [], []
        for fc in range(KF):
            sm_.append(("g1", fc)); sm_.append(("u1", fc))
        for dc in range(KD):
            bg_.append(("d1", dc))
        for j in range(12):
            sm_.append(("xbc", j))
        for i in range(6):
            bg_.append(("uvz", i))
        for dc in range(KD):
            bg_.append(("wo", dc))
        for fc in range(KF):
            sm_.append(("g2", fc)); sm_.append(("u2", fc))
        for dc in range(KD):
            bg_.append(("d2", dc))
        for dc in range(KD):
            sm_.append(("pg", dc))
            sm_.append(("pp", dc))
        return sm_, bg_
    for n in range(nt):
        a, b = seq_for_tile()
        sm_uses += a
        bg_uses += b
    WSZ = {"g1": 1024, "u1": 1024, "g2": 1024, "u2": 1024, "xbc": 1024, "pg": 1024, "pp": 256,
           "d1": KF * 128, "d2": KF * 128, "uvz": 4096, "wo": 2048}

    class Ring:
        def __init__(self, name, slots, uses):
            self.name, self.slots, self.uses = name, slots, uses
            self.next_fetch = 0
            self.next_take = 0

        def fetch(self):
            i = self.next_fetch
            if i >= len(self.uses):
                return
            kind, idx = self.uses[i]
            slot = self.slots[i % len(self.slots)]
            n_ = WSZ[kind]
            keys = [("wo_part", k) for k in range(16)] if kind == "wo" else [(kind, idx)]
            S.dma("sp", f"{self.name}{i % len(self.slots)}", slot[:, 0:n_], V(sc[kind][idx], keys))
            self.next_fetch += 1

        def take(self, kind, idx):
            i = self.next_take
            while self.uses[i] != (kind, idx):
                assert dbg, (self.uses[i], kind, idx)
                i += 1
                self.next_fetch = max(self.next_fetch, i)
            self.next_take = i
            self.next_take += 1
            while self.next_fetch < min(len(self.uses), i + len(self.slots) - 1):
                self.fetch()
            return self.slots[i % len(self.slots)]

    smR = Ring("smr", smr, sm_uses)
    bgR = Ring("bgr", bgr, bg_uses)

    def rmsnorm_T(kind):
        b = bank()
        for kc in range(KD):
            act(sq[kc % 2][:, :], hT.v((slice(None), kc, slice(None)), [("hT", kc)]), AF.Square)
            mm(b[:, :], onesb[:, :], sq[kc % 2][:, :], start=(kc == 0), stop=(kc == KD - 1))
        act(srt[:, :], b[:, :], AF.Sqrt, bias=epsb[:, 0:1], scale=1.0 / D)
        recip(rstd[:, :], srt[:, :])

    def hTv(kc, cols=slice(None)):
        return hT.v((slice(None), kc, cols), [("hT", kc)])

    def nTv(kc, cols=slice(None)):
        return nT.v((slice(None), kc, cols), [("nT", kc)])

    def Gv(i, cols=slice(None)):
        return G.v((slice(None), i, cols), [("G", i)])

    epsb = sb("epsb", [128, 2], F32)
    memset("pool", epsb[:, 0:1], EPS)
    memset("pool", epsb[:, 1:2], 1.0)

    def norm_to_nT(kind):
        rmsnorm_T(kind)
        for kc in range(KD):
            stt(nTv(kc), hTv(kc), vecT[:, kind, kc:kc + 1], rstd[:, :], ALU.mult, ALU.mult)

    def ffn(f, kind):
        norm_to_nT(kind)
        for fc in range(KF):
            wg = smR.take(f"g{f}", fc)
            wu = smR.take(f"u{f}", fc)
            bg_ = bank()
            for kc in range(KD):
                mm(bg_[:, :], wg[:, kc * 128:(kc + 1) * 128], nTv(kc), start=(kc == 0), stop=(kc == KD - 1))
            bu_ = bank()
            for kc in range(KD):
                mm(bu_[:, :], wu[:, kc * 128:(kc + 1) * 128], nTv(kc), start=(kc == 0), stop=(kc == KD - 1))
            act(sgt[fc % 2][:, :], bg_[:, :], AF.Silu)
            tt("dve", Gv(fc), bu_[:, :], sgt[fc % 2][:, :], ALU.mult)
        for dc in range(KD):
            wd = bgR.take(f"d{f}", dc)
            bd = bank()
            for fc in range(KF):
                mm(bd[:, :], wd[:, fc * 128:(fc + 1) * 128], Gv(fc), start=(fc == 0), stop=(fc == KF - 1))
            stt(hTv(dc), bd[:, :], 0.5, hTv(dc), ALU.mult, ALU.add)

    def load_x(n):
        for c in range(NCHK):
            r0 = n * T + c * 128
            xin = io[c % 2]
            S.dma("sp", f"io{c % 2}", xin[:, :], dv(x_d[r0:r0 + 128, :], "x_dram"))
            cols = slice(c * 128, (c + 1) * 128)
            for half in range(2):
                b = bank()
                for j in range(4):
                    kc = half * 4 + j
                    tr(b[:, j * 128:(j + 1) * 128], xin[:, kc * 128:(kc + 1) * 128], identf[:, :])
                keys = [("hT", half * 4 + j) for j in range(4)]
                cp("act" if half == 0 else "dve", hT.v((slice(None), slice(half * 4, half * 4 + 4), cols), keys),
                   b[:, :].re("p (j t) -> p j t", j=4))

    def load_p(n):
        for c in range(NCHK):
            r0 = n * T + c * 128
            pi = pin[c % 2]
            S.dma("sp", f"pin{c % 2}", pi[:, :], dv(p_d[r0:r0 + 128, :], "p_dram"))
            b = bank()
            for kc in range(2):
                tr(b[:, kc * 128:(kc + 1) * 128], pi[:, kc * 128:(kc + 1) * 128], identf[:, :])
            cp("act", pT[:, :, c * 128:(c + 1) * 128], b[:, 0:256].re("p (j t) -> p j t", j=2))

    def mixer(n):
        norm_to_nT(1)
        for j in range(12):
            w = smR.take("xbc", j)
            b = bank()
            for kc in range(KD):
                mm(b[:, :], w[:, kc * 128:(kc + 1) * 128], nTv(kc), start=(kc == 0), stop=(kc == KD - 1))
            pr = pre[j % 2]
            cp("pool", pr[:, 0:3], halo[:, j, :])
            cp("act", pr[:, 3:T + 3], b[:, :])
            cp("pool", halo[:, j, :], pr[:, T:T + 3])
            ac = cacc[j % 2]
            ts("dve", ac[:, :], pr[:, 0:T], cw[:, j, 0:1], cb[:, j:j + 1], ALU.mult, ALU.add)
            for k in range(1, 4):
                stt(ac[:, :], pr[:, k:k + T], cw[:, j, k:k + 1], ac[:, :], ALU.mult, ALU.add)
            act(xbcT[:, j, :], ac[:, :], AF.Silu)
        def proj(w, c, i_half, evac):
            b = bank()
            for kc in range(KD):
                mm(b[:, :], nTv(kc, slice(c * 128, (c + 1) * 128)), w[:, kc * 512:(kc + 1) * 512],
                   start=(kc == 0), stop=(kc == KD - 1))
            evac(b, c, i_half)

        def ev_u(b, c, ih):
            act(Gv(8 + 2 * c + ih), b[:, :], AF.Gelu)

        def ev_z(b, c, ih):
            act(zB[:, c, ih * 512:(ih + 1) * 512], b[:, :], AF.Silu)

        def ev_v(b, c, ih):
            act(v_g[:, ih * 512:(ih + 1) * 512], b[:, :], AF.Gelu)
            o_, i_ = st6[:, ih, :], v_g[:, ih * 512:(ih + 1) * 512]
            S.op("dve", lambda e: e.bn_stats(out=o_.ap, in_=i_.ap), outs=[o_], ins=[i_])

        for ih in range(2):
            w = bgR.take("uvz", ih)
            for c in range(NCHK):
                proj(w, c, ih, ev_u)
        wv0 = bgR.take("uvz", 2)
        wv1 = bgR.take("uvz", 3)
        for c in range(NCHK):
            proj(wv0, c, 0, ev_v)
            proj(wv1, c, 1, ev_v)
            o_, i_ = mv[:, :], st6[:, :, :].re("p a b -> p (a b)")
            S.op("dve", (lambda o_=o_, i_=i_: lambda e: e.bn_aggr(out=o_.ap, in_=i_.ap))(), outs=[o_], ins=[i_])
            act(s1["lnr"][:, 0:1], mv[:, 1:2], AF.Sqrt, bias=epsb[:, 0:1], scale=1.0)
            recip(s1["lnr2"][:, 0:1], s1["lnr"][:, 0:1])
            stt(s1["lnb"][:, 0:1], mv[:, 0:1], -1.0, s1["lnr2"][:, 0:1], ALU.mult, ALU.mult)
            for ih in range(2):
                act(Gv(16 + 2 * c + ih), v_g[:, ih * 512:(ih + 1) * 512], AF.Identity,
                    bias=s1["lnb"][:, 0:1], scale=s1["lnr2"][:, 0:1])
        for ih in range(2):
            w = bgR.take("uvz", 4 + ih)
            for c in range(NCHK):
                proj(w, c, ih, ev_z)
        bdts = []
        for c in range(NCHK):
            b = bank()
            for kc in range(KD):
                mm(b[:, 0:16], nTv(kc, slice(c * 128, (c + 1) * 128)), wdt[:, kc * 16:(kc + 1) * 16],
                   start=(kc == 0), stop=(kc == KD - 1))
            bdts.append(b)
            cp("dve", dtpre[:, c, :], b[:, 0:16])

        for c in range(NCHK):
            cols = slice(c * 128, (c + 1) * 128)
            bs_ = [bank(), bank()]
            for h in range(8):
                vh = Gv(16 + 2 * c + h // 4, slice((h % 4) * 128, (h % 4 + 1) * 128))
                mm(bs_[h // 4][:, (h % 4) * 128:(h % 4 + 1) * 128], WsT[:, h, :], vh)
            for ih in range(2):
                fs = slice(ih * 512, (ih + 1) * 512)
                tt("dve", mx[:, fs], bs_[ih][:, :], glnb[:, fs], ALU.mult)
                tt("pool", mx[:, fs], mx[:, fs], CstT[:, fs], ALU.add)
                tt("pool", mx[:, fs], mx[:, fs], Gv(8 + 2 * c + ih), ALU.mult)
            act(junk[:, :], mx[:, :], AF.Square, accum=s1["ssa"][:, 0:1])
            act(s1["ra"][:, 0:1], s1["ssa"][:, 0:1], AF.Sqrt, bias=epsb[:, 0:1], scale=1.0 / 1024)
            recip(s1["ra2"][:, 0:1], s1["ra"][:, 0:1])
            act(yan[:, :], mx[:, :], AF.Copy, scale=s1["ra2"][:, 0:1])
            bt = bank()
            btb = bt[:, :].bitcast(BF16)
            for kc in range(8):
                tr(btb[:, kc * 128:(kc + 1) * 128], yan[:, kc * 128:(kc + 1) * 128], identb[:, :])
            cp("dve", nT.v((slice(None), slice(0, 8), cols), [("nT", k) for k in range(8)]),
               btb[:, :].re("p (j t) -> p j t", j=8))
            bt2 = bank()
            bt2b = bt2[:, :].bitcast(BF16)
            for j in range(8):
                tr(bt2b[:, j * 128:(j + 1) * 128], xbcT[:, j, cols], identb[:, :])
            cp("act", xs_tok[:, :], bt2b[:, :])
            bt3 = bank()
            bt3b = bt3[:, :].bitcast(BF16)
            for g in range(2):
                tr(bt3b[:, g * 128:(g + 1) * 128], xbcT[:, 8 + g, cols], identb[:, :])
            cp("act", Btok[:, :], bt3b[:, 0:256])
            tt("dve", sm["t16a"][:, :], dtpre[:, c, :], dtb[:, :], ALU.add)
            act(sm["t16b"][:, :], sm["t16a"][:, :], AF.Exp)
            act(sm["dt"][:, :], sm["t16b"][:, :], AF.Ln, bias=epsb[:, 1:2], scale=1.0)
            tt("dve", sm["adt"][:, :], sm["dt"][:, :], abc[:, :], ALU.mult)
            bq = bank()
            mm(bq[:, 0:16], mle[:, :], sm["adt"][:, :])
            mm(bq[:, 16:32], onesf[:, :], sm["adt"][:, :])
            mm(bq[0:16, 32:160], sm["adt"][:, :], mle[:, :])
            act(sm["nacs"][:, :], bq[:, 0:16], AF.Copy, scale=-1.0)
            act(sm["eacs"][:, :], bq[:, 0:16], AF.Exp)
            tt("dve", sm["t16c"][:, :], bq[:, 16:32], sm["nacs"][:, :], ALU.add)
            act(sm["ds"][:, :], sm["t16c"][:, :], AF.Exp)
            tt("dve", sm["dtds"][:, :], sm["dt"][:, :], sm["ds"][:, :], ALU.mult)
            act(sm["etot"][:, :], bq[:, 16:32], AF.Exp)
            cp("dve", acsT[:, :], bq[0:16, 32:160])
            xs3 = xs_tok[:, :].re("p (h q) -> p h q", h=NH)
            tt("dve", xdt[:, :].re("p (h q) -> p h q", h=NH), xs3, sm["dt"][:, :].re("p (h o) -> p h o", o=1).bcast([128, NH, HP]), ALU.mult)
            tt("pool", xdtds[:, :].re("p (h q) -> p h q", h=NH), xs3, sm["dtds"][:, :].re("p (h o) -> p h o", o=1).bcast([128, NH, HP]), ALU.mult)
            bc_ = bank()
            for g in range(2):
                mm(bc_[:, g * 128:(g + 1) * 128], xbcT[:, 8 + g, cols], xbcT[:, 10 + g, cols])
                tt("dve", cbm[:, g, :], bc_[:, g * 128:(g + 1) * 128], mle[:, :], ALU.mult)
            for hq in range(4):
                bk = bank()
                for j in range(4):
                    h = hq * 4 + j
                    mm(bk[:, j * 128:(j + 1) * 128], sel[:, h, :], acsT[:, :], start=True, stop=False)
                    mm(bk[:, j * 128:(j + 1) * 128], identb[:, :], negm[:, :], start=False, stop=True)
                for j in range(4):
                    h = hq * 4 + j
                    act(decT[:, h, :], bk[:, j * 128:(j + 1) * 128], AF.Exp, bias=sm["nacs"][:, h:h + 1], scale=1.0)
            for g in range(2):
                tt("dve", MT[:, 8 * g:8 * g + 8, :], decT[:, 8 * g:8 * g + 8, :],
                   cbm[:, g:g + 1, :].bcast([128, 8, 128]), ALU.mult)
            by = [bank(), bank()]
            for h in range(NH):
                mm(by[h // 8][:, (h % 8) * 64:(h % 8 + 1) * 64], MT[:, h, :], xdt[:, h * 64:(h + 1) * 64])
            bo = [bank(), bank()]
            for g in range(2):
                mm(bo[g][:, :], xbcT[:, 10 + g, cols], Sbf[:, g * 512:(g + 1) * 512])
            for g in range(2):
                fs = slice(g * 512, (g + 1) * 512)
                tt("dve", y1[:, fs].re("p (h q) -> p h q", h=8), bo[g][:, :].re("p (h q) -> p h q", h=8),
                   sm["eacs"][:, 8 * g:8 * g + 8].re("p (h o) -> p h o", o=1).bcast([128, 8, HP]), ALU.mult)
                tt("dve", y1[:, fs], by[g][:, :], y1[:, fs], ALU.add)
                tt("pool", y2[:, fs].re("p (h q) -> p h q", h=8), xs_tok[:, fs].re("p (h q) -> p h q", h=8),
                   dsk[:, 8 * g:8 * g + 8].re("p (h o) -> p h o", o=1).bcast([128, 8, HP]), ALU.mult)
                tt("pool", y2[:, fs], y2[:, fs], y1[:, fs], ALU.add)
                tt("pool", y2[:, fs], y2[:, fs], zB[:, c, fs], ALU.mult)
                act(junk[:, fs], y2[:, fs], AF.Square, accum=s1["ssb"][:, g:g + 1])
            act(s1["rb"][:, 0:2], s1["ssb"][:, 0:2], AF.Sqrt, bias=epsb[:, 0:1], scale=1.0 / 512)
            recip(s1["rb2"][:, 0:2], s1["rb"][:, 0:2])
            for g in range(2):
                fs = slice(g * 512, (g + 1) * 512)
                act(ybn[:, fs], y2[:, fs], AF.Copy, scale=s1["rb2"][:, g:g + 1])
            bt4 = bank()
            bt4b = bt4[:, :].bitcast(BF16)
            for kc in range(8):
                tr(bt4b[:, kc * 128:(kc + 1) * 128], ybn[:, kc * 128:(kc + 1) * 128], identb[:, :])
            cp("act", G.v((slice(None), slice(0, 8), cols), [("G", k) for k in range(8)]),
               bt4b[:, :].re("p (j t) -> p j t", j=8))
            bst = [bank(), bank()]
            for g in range(2):
                mm(bst[g][:, :], Btok[:, g * 128:(g + 1) * 128], xdtds[:, g * 512:(g + 1) * 512])
            for g in range(2):
                fs = slice(g * 512, (g + 1) * 512)
                tt("pool", Sst[:, fs].re("p (h q) -> p h q", h=8), Sst[:, fs].re("p (h q) -> p h q", h=8),
                   sm["etot"][:, 8 * g:8 * g + 8].re("p (h o) -> p h o", o=1).bcast([128, 8, HP]), ALU.mult)
                tt("dve", Sst[:, fs], bst[g][:, :], Sst[:, fs], ALU.add)
                cp("act", Sbf[:, fs], Sst[:, fs])
        for dc in range(KD):
            w = bgR.take("wo", dc)
            b = bank()
            for kc in range(16):
                rhs = nTv(kc) if kc < 8 else Gv(kc - 8)
                mm(b[:, :], w[:, kc * 128:(kc + 1) * 128], rhs, start=(kc == 0), stop=(kc == 15))
            tt("dve", hTv(dc), b[:, :], hTv(dc), ALU.add)

    dtpre = sb("dtpre", [128, NCHK, 16], F32)

    def ple(n):
        load_p(n)
        norm_to_nT(3)
        wp = None
        for dc in range(KD):
            w = smR.take("pg", dc)
            wp = smR.take("pp", dc)
            b = bank()
            for kc in range(KD):
                mm(b[:, :], w[:, kc * 128:(kc + 1) * 128], nTv(kc), start=(kc == 0), stop=(kc == KD - 1))
            g_ = gsb[dc % 2]
            act(g_[:, :], b[:, :], AF.Sigmoid, bias=vecT[:, 5, dc:dc + 1], scale=1.0)
            b2 = bank()
            for kc in range(2):
                mm(b2[:, :], wp[:, kc * 128:(kc + 1) * 128], pT[:, kc, :], start=(kc == 0), stop=(kc == 1))
            tt("dve", g_[:, :], b2[:, :], g_[:, :], ALU.mult)
            tt("pool", hTv(dc), hTv(dc), g_[:, :], ALU.add)

    def final_and_store(n, do_norm=True):
        if do_norm:
            rmsnorm_T(4)
            for kc in range(KD):
                stt(hTv(kc), hTv(kc), vecT[:, 4, kc:kc + 1], rstd[:, :], ALU.mult, ALU.mult)
        for c in range(NCHK):
            cols = slice(c * 128, (c + 1) * 128)
            ot = io[c % 2]
            for half in range(2):
                b = bank()
                for j in range(4):
                    kc = half * 4 + j
                    tr(b[:, j * 128:(j + 1) * 128], hTv(kc, cols), identf[:, :])
                cp("act" if half == 0 else "dve", ot[:, half * 512:(half + 1) * 512], b[:, :])
            r0 = n * T + c * 128
            S.dma("sp", f"io{c % 2}", dv(out_d[r0:r0 + 128, :], ("out", n, c)), ot[:, :])

    conv_late()
    for n in range(nt):
        load_x(n)
        if dbg == "x":
            final_and_store(n, False); continue
        ffn(1, 0)
        if dbg == "ffn1":
            final_and_store(n, False); continue
        if n == 0:
            S.dma("sp", "wdt", wdt[:, :], dv(sc["dt"], ("dt", 0)))
        mixer(n)
        if dbg == "mix":
            final_and_store(n, False); continue
        ffn(2, 2)
        if dbg == "ffn2":
            final_and_store(n, False); continue
        ple(n)
        if dbg == "ple":
            final_and_store(n, False); continue
        final_and_store(n)
    S.final_wait_all("sp")

    sems = {e: es.enter_context(nc.semaphore("s_" + e)) for e in ENGS}
    dma_sems = {k: es.enter_context(nc.semaphore("d_" + k)) for k in S.dma_cnt}
    block = es.enter_context(nc.Block())
    S.emit_all(nc, block, sems, dma_sems)
    es.close()
    return nc


_NC_CACHE = {}


def kernel(**inputs):
    nt = SEQ // T
    if nt not in _NC_CACHE:
        _NC_CACHE[nt] = build_nc(nt)
    nc = _NC_CACHE[nt]
    x = np.ascontiguousarray(inputs["x"], dtype=np.float32)
    p = np.ascontiguousarray(inputs["p"], dtype=np.float32)
    wmap = {n: np.ascontiguousarray(inputs[n], dtype=np.float32) for n in WNAMES}
    in_maps = []
    for c in range(N_CORES):
        m = {"x": x[c], "p": p[0, c]}
        m.update(wmap)
        in_maps.append(m)
    res = run_bass_kernel_spmd(nc, in_maps, core_ids=list(range(N_CORES)))
    return np.stack([np.asarray(r["out"], dtype=np.float32) for r in res.results], axis=0)
```

```python
import numpy as np
from contextlib import ExitStack
import concourse.bass as bass
import concourse.mybir as mybir
from concourse.bass_utils import run_bass_kernel_spmd

F32 = mybir.dt.float32
BF16 = mybir.dt.bfloat16
AF = mybir.ActivationFunctionType
ALU = mybir.AluOpType
AX = mybir.AxisListType

D = 1024
KD = 8
DFF = 2816
KF = 22
T = 512
NCHK = 4
SEQ = 8192
NH = 16
HP = 64
DIN = 4624
EPS = 1e-6
N_CORES = 8


class V:
    __slots__ = ("ap", "k")

    def __init__(self, ap, k):
        self.ap = ap
        self.k = tuple(k)

    def __getitem__(self, idx):
        return V(self.ap[idx], self.k)

    def re(self, pat, **kw):
        return V(self.ap.rearrange(pat, **kw), self.k)

    def bcast(self, shape):
        return V(self.ap.broadcast_to(shape), self.k)

    def bitcast(self, dt):
        return V(self.ap.bitcast(dt), self.k)

    def keys(self, k):
        return V(self.ap, k)


class Tl:
    def __init__(self, handle, key):
        self.t = handle
        self.key = key

    def __getitem__(self, idx):
        return V(self.t[idx], (self.key,))

    def v(self, idx, keys):
        return V(self.t[idx], keys)


ENGS = ("pe", "act", "dve", "pool", "sp")


class Sched:
    def __init__(self):
        self.ops = {e: [] for e in ENGS}
        self.last_w = {}
        self.readers = {}
        self.dma_cnt = {}
        self.dma_group = set()

    def _deps(self, eng, reads, writes):
        deps = set()
        for b in reads:
            if b in self.last_w:
                d = self.last_w[b]
                deps.add(d)
        for b in writes:
            if b in self.last_w:
                d = self.last_w[b]
                deps.add(d)
            for d in self.readers.get(b, ()):
                deps.add(d)
        out = []
        for d in deps:
            if d[0] == "dma":
                out.append(d)
            else:
                e2, i2 = d
                if e2 == eng and eng in ("pe", "sp"):
                    continue
                self.ops[e2][i2]["sig"] = True
                out.append(d)
        return out

    def op(self, eng, emit, outs=(), ins=()):
        reads = [k for v in ins for k in v.k]
        writes = [k for v in outs for k in v.k]
        deps = self._deps(eng, reads, writes)
        idx = len(self.ops[eng])
        self.ops[eng].append(dict(emit=emit, deps=deps, sig=False, dma=None))
        tok = (eng, idx)
        for b in writes:
            self.last_w[b] = tok
            self.readers[b] = []
        for b in reads:
            if b not in writes:
                self.readers.setdefault(b, []).append(tok)
        return tok

    def dma(self, queue, semkey, out, in_, group=False, **kw):
        reads = list(in_.k)
        writes = list(out.k)
        deps = self._deps(queue, reads, writes)
        if group:
            deps = [d for d in deps if not (d[0] == "dma" and d[1] == semkey)]
        cnt = self.dma_cnt.get(semkey, 0) + 1
        self.dma_cnt[semkey] = cnt
        if group:
            self.dma_group.add(semkey)
        o_ap, i_ap = out.ap, in_.ap
        idx = len(self.ops[queue])
        self.ops[queue].append(dict(emit=lambda e: e.dma_start(out=o_ap, in_=i_ap, **kw), deps=deps, sig=False, dma=semkey))
        tok = ("dma", semkey, cnt)
        for b in writes:
            self.last_w[b] = tok
            self.readers[b] = []
        for b in reads:
            self.readers.setdefault(b, []).append(tok)
        return tok

    def final_wait_all(self, eng):
        deps = []
        for semkey, cnt in self.dma_cnt.items():
            deps.append(("dma", semkey, cnt))
        self.ops[eng].append(dict(emit=None, deps=deps, sig=False, dma=None))

    def emit_all(self, nc, block, sems, dma_sems):
        signum = {}
        for e in ENGS:
            n = 0
            m = {}
            for i, o in enumerate(self.ops[e]):
                if o["sig"]:
                    n += 1
                    m[i] = n
            signum[e] = m

        def run(eng_name, eng):
            waited = {}
            for o in self.ops[eng_name]:
                need = {}
                for d in o["deps"]:
                    if d[0] == "dma":
                        _, semkey, cnt = d
                        if semkey in self.dma_group:
                            cnt = self.dma_cnt[semkey]
                        key = ("dma", semkey)
                        val = 16 * cnt
                        sem = dma_sems[semkey]
                    else:
                        e2, i2 = d
                        key = ("eng", e2)
                        val = signum[e2][i2]
                        sem = sems[e2]
                    if waited.get(key, 0) >= val:
                        continue
                    if key not in need or need[key][1] < val:
                        need[key] = (sem, val)
                for key, (sem, val) in need.items():
                    eng.wait_ge(sem, val)
                    waited[key] = val
                if o["emit"] is None:
                    continue
                ins = o["emit"](eng)
                if o["dma"] is not None:
                    ins.then_inc(dma_sems[o["dma"]], 16)
                elif o["sig"]:
                    ins.then_inc(sems[eng_name], 1)

        @block.tensor
        def _(e):
            run("pe", e)

        @block.scalar
        def _(e):
            run("act", e)

        @block.vector
        def _(e):
            run("dve", e)

        @block.gpsimd
        def _(e):
            run("pool", e)

        @block.sync
        def _(e):
            run("sp", e)


WNAMES = ["ffn1_norm", "ffn1_w_gate", "ffn1_w_up", "ffn1_w_down", "mix_norm", "w_in", "gm_ln_g", "gm_ln_b",
          "gm_w_s", "gm_b_s", "gm_out_norm", "conv_w", "conv_b", "dt_bias", "a_log", "d_skip", "ssm_norm",
          "w_out", "ffn2_norm", "ffn2_w_gate", "ffn2_w_up", "ffn2_w_down", "ple_norm", "ple_w_gate",
          "ple_b_gate", "ple_w_proj", "final_norm"]
WSHAPES = {
    "ffn1_norm": [1, 1024], "ffn1_w_gate": [1, 1024, 2816], "ffn1_w_up": [1, 1024, 2816], "ffn1_w_down": [1, 2816, 1024],
    "mix_norm": [1, 1024], "w_in": [1, 1024, 4624], "gm_ln_g": [1, 1024], "gm_ln_b": [1, 1024],
    "gm_w_s": [1, 8, 128, 128], "gm_b_s": [1, 8, 128], "gm_out_norm": [1, 1024], "conv_w": [1, 4, 1536],
    "conv_b": [1, 1536], "dt_bias": [1, 16], "a_log": [1, 16], "d_skip": [1, 16], "ssm_norm": [1, 1024],
    "w_out": [1, 2048, 1024], "ffn2_norm": [1, 1024], "ffn2_w_gate": [1, 1024, 2816], "ffn2_w_up": [1, 1024, 2816],
    "ffn2_w_down": [1, 2816, 1024], "ple_norm": [1, 1024], "ple_w_gate": [1, 1024, 1024], "ple_b_gate": [1, 1024],
    "ple_w_proj": [1, 256, 1024], "final_norm": [1024],
}


def build_nc(nt=SEQ // T, dbg=False):
    seq = nt * T
    nc = bass.Bass("TRN2", target_bir_lowering=False)
    S = Sched()
    es = ExitStack()

    def dram(name, shape, dt, kind):
        return nc.dram_tensor(name, shape, dt, kind=kind).ap()

    x_d = dram("x", [seq, D], F32, "ExternalInput")
    p_d = dram("p", [seq, 256], F32, "ExternalInput")
    out_d = dram("out", [seq, D], F32, "ExternalOutput")
    W = {n: dram(n, WSHAPES[n], F32, "ExternalInput") for n in WNAMES}

    sc = {}
    for f in (1, 2):
        sc[f"g{f}"] = dram(f"sc_g{f}", [KF, 128, KD * 128], BF16, "Internal")
        sc[f"u{f}"] = dram(f"sc_u{f}", [KF, 128, KD * 128], BF16, "Internal")
        sc[f"d{f}"] = dram(f"sc_d{f}", [KD, 128, KF * 128], BF16, "Internal")
    sc["uvz"] = dram("sc_uvz", [6, 128, KD * 512], BF16, "Internal")
    sc["xbc"] = dram("sc_xbc", [12, 128, KD * 128], BF16, "Internal")
    sc["dt"] = dram("sc_dt", [128, KD * 16], BF16, "Internal")
    sc["wo"] = dram("sc_wo", [KD, 128, 16 * 128], BF16, "Internal")
    sc["pg"] = dram("sc_pg", [KD, 128, KD * 128], BF16, "Internal")
    sc["pp"] = dram("sc_pp", [KD, 128, 2 * 128], BF16, "Internal")

    def sb(name, shape, dt, key=None):
        h = es.enter_context(nc.sbuf_tensor(name, shape, dt))
        return Tl(h, key or name)

    hT = sb("hT", [128, KD, T], F32)
    nT = sb("nT", [128, KD, T], BF16)
    G = sb("G", [128, 24, T], BF16)
    srt = sb("srt", [128, T], F32)
    rstd = sb("rstd", [128, T], F32)
    sgt = [sb(f"sgt{i}", [128, T], BF16) for i in range(2)]
    sq = sgt
    NSM, NBG = 6, 3
    smr = [sb(f"smr{i}", [128, 1024], BF16) for i in range(NSM)]
    bgr = [sb(f"bgr{i}", [128, 4096], BF16) for i in range(NBG)]
    wdt = sb("wdt", [128, KD * 16], BF16)
    xbcT = sb("xbcT", [128, 12, T], BF16)
    pre = [sb(f"pre{i}", [128, T + 3], F32) for i in range(2)]
    halo = sb("halo", [128, 12, 3], F32)
    cacc = [sb(f"cacc{i}", [128, T], F32) for i in range(2)]
    Sst = sb("Sst", [128, 1024], F32)
    Sbf = sb("Sbf", [128, 1024], BF16)
    io = [sb(f"io{i}", [128, 1024], F32) for i in range(2)]
    pin = [sb(f"pin{i}", [128, 256], F32) for i in range(2)]
    pT = sb("pT", [128, 2, T], BF16)
    WsT = sb("WsT", [128, 8, 128], BF16)
    CstT = sb("CstT", [128, 1024], F32)
    glnb = sb("glnb", [128, 1024], F32)
    identb = sb("identb", [128, 128], BF16)
    identf = sb("identf", [128, 128], F32)
    mle = sb("mle", [128, 128], F32)
    mge = sb("mge", [128, 128], F32)
    negm = sb("negm", [128, 128], BF16)
    onesf = sb("onesf", [128, 128], F32)
    onesb = sb("onesb", [128, 128], BF16)
    vecT = sb("vecT", [128, 8, 8], F32)
    cw = sb("cw", [128, 12, 4], F32)
    cb = sb("cb", [128, 12], F32)
    dtb = sb("dtb", [128, 16], F32)
    abc = sb("abc", [128, 16], F32)
    dsk = sb("dsk", [128, 16], F32)
    bsT = sb("bsT", [128, 8], F32)
    rw = sb("rw", [128, 8], F32)
    zB = sb("zB", [128, NCHK, 1024], BF16)
    mx = [sb(f"mx{i}", [128, 1024], F32) for i in range(2)]
    yan = [sb(f"yan{i}", [128, 1024], BF16) for i in range(2)]
    xs_tok = [sb(f"xs_tok{i}", [128, 1024], BF16) for i in range(2)]
    xdt = [sb(f"xdt{i}", [128, 1024], BF16) for i in range(2)]
    xdtds = [sb(f"xdtds{i}", [128, 1024], BF16) for i in range(2)]
    Btok = [sb(f"Btok{i}", [128, 256], BF16) for i in range(2)]
    decT = [sb(f"decT{i}", [128, 16, 128], BF16) for i in range(2)]
    cbm = [sb(f"cbm{i}", [128, 2, 128], F32) for i in range(2)]
    y1 = [sb(f"y1{i}", [128, 1024], F32) for i in range(2)]
    y2 = sb("y2", [128, 1024], F32)
    Rb = [sb(f"Rb{i}", [128, 128], F32) for i in range(4)]
    ybn = [sb(f"ybn{i}", [128, 1024], BF16) for i in range(2)]
    v_g = y1
    st6 = sb("st6", [128, 2, 6], F32)
    mv = sb("mv", [128, 2], F32)
    sm = {n: sb("sm_" + n, [128, NCHK, 16], F32) for n in
          ["t16a", "t16b", "dt", "adt", "nacs", "eacs", "t16c", "ds", "dtds", "etot"]}
    s1 = {n: sb("s1_" + n, [128, 4], F32) for n in ["lnr", "lnr2", "lnb", "ssa", "ra", "ra2", "ssb", "rb", "rb2"]}

    banks = [Tl(es.enter_context(nc.psum_tensor(f"bank{i}", [128, 512], F32)), f"bank{i}") for i in range(8)]
    bank_ctr = [0]

    def bank():
        b = banks[bank_ctr[0] % 8]
        bank_ctr[0] += 1
        assert not (b.key in S.last_w and not S.readers.get(b.key)), f"PSUM {b.key} re-allocated before its reader was emitted"
        return b

    def mm(out, lhsT, rhs, start=True, stop=True):
        o, l, r = out.ap, lhsT.ap, rhs.ap
        S.op("pe", lambda e: e.matmul(o, lhsT=l, rhs=r, start=start, stop=stop), outs=[out], ins=[lhsT, rhs])

    def tr(out, in_, ident):
        o, i, d = out.ap, in_.ap, ident.ap
        S.op("pe", lambda e: e.transpose(o, i, d), outs=[out], ins=[in_, ident])

    def act(out, in_, func, bias=None, scale=None, accum=None, eng="act"):
        o, i = out.ap, in_.ap
        kw = {}
        ins = [in_]
        outs = [out]
        if bias is not None:
            if isinstance(bias, V):
                kw["bias"] = bias.ap
                ins.append(bias)
            else:
                kw["bias"] = bias
        if scale is not None:
            if isinstance(scale, V):
                kw["scale"] = scale.ap
                ins.append(scale)
            else:
                kw["scale"] = scale
        if accum is not None:
            kw["accum_out"] = accum.ap
            outs.append(accum)
        S.op("act", lambda e: e.activation(out=o, in_=i, func=func, **kw), outs=outs, ins=ins)

    def tt(eng, out, in0, in1, op):
        o, a, b = out.ap, in0.ap, in1.ap
        S.op(eng, lambda e: e.tensor_tensor(out=o, in0=a, in1=b, op=op), outs=[out], ins=[in0, in1])

    def ts(eng, out, in0, s1_, s2_, op0, op1=None):
        o, a = out.ap, in0.ap
        ins = [in0]
        a1 = s1_
        a2 = s2_
        if isinstance(s1_, V):
            ins.append(s1_)
            a1 = s1_.ap
        if isinstance(s2_, V):
            ins.append(s2_)
            a2 = s2_.ap
        if op1 is None:
            S.op(eng, lambda e: e.tensor_scalar(out=o, in0=a, scalar1=a1, scalar2=None, op0=op0), outs=[out], ins=ins)
        else:
            S.op(eng, lambda e: e.tensor_scalar(out=o, in0=a, scalar1=a1, scalar2=a2, op0=op0, op1=op1), outs=[out], ins=ins)

    def stt(out, in0, scalar, in1, op0, op1):
        o, a, b = out.ap, in0.ap, in1.ap
        ins = [in0, in1]
        sc_ = scalar
        if isinstance(scalar, V):
            ins.append(scalar)
            sc_ = scalar.ap
        S.op("dve", lambda e: e.scalar_tensor_tensor(out=o, in0=a, scalar=sc_, in1=b, op0=op0, op1=op1), outs=[out], ins=ins)

    def cp(eng, out, in_):
        o, i = out.ap, in_.ap
        if eng == "act":
            S.op("act", lambda e: e.activation(out=o, in_=i, func=AF.Copy), outs=[out], ins=[in_])
        else:
            S.op(eng, lambda e: e.tensor_copy(out=o, in_=i), outs=[out], ins=[in_])

    def recip(out, in_):
        o, i = out.ap, in_.ap
        S.op("dve", lambda e: e.reciprocal(out=o, in_=i), outs=[out], ins=[in_])

    def memset(eng, out, val):
        o = out.ap
        S.op(eng, lambda e: e.memset(o, val), outs=[out])

    def dv(ap, key):
        return V(ap, (key,))

    pro = "pro"

    def pload(out, in_ap, key="w_in_dram"):
        S.dma("sp", pro, out, dv(in_ap, key), group=True, allow_slow_non_contiguous=True)

    for i, n in enumerate(["ffn1_norm", "mix_norm", "ffn2_norm", "ple_norm", "final_norm", "ple_b_gate", "gm_out_norm", "ssm_norm"]):
        src = W[n] if n == "final_norm" else W[n][0]
        pload(vecT[:, i, :], src.rearrange("(kc p) -> p kc", p=128))
    for j in range(12):
        pload(cw[:, j, :], W["conv_w"][0][:, j * 128:(j + 1) * 128].rearrange("k c -> c k"))
    pload(cb[:, :], W["conv_b"][0].rearrange("(j c) -> c j", c=128))
    pload(dtb[:, :], W["dt_bias"][0:1, :].broadcast_to([128, 16]))
    pload(abc[:, :], W["a_log"][0:1, :].broadcast_to([128, 16]))
    pload(dsk[:, :], W["d_skip"][0:1, :].broadcast_to([128, 16]))
    pload(bsT[:, :], W["gm_b_s"][0].rearrange("h t -> t h"))
    pload(glnb[:, :], W["gm_ln_g"][0:1, :].broadcast_to([128, 1024]))
    pload(y2[:, :], W["gm_ln_b"][0:1, :].broadcast_to([128, 1024]))

    memset("pool", onesf[:, :], 1.0)
    memset("pool", onesb[:, :], 1.0)
    memset("pool", halo[:, :, :], 0.0)
    memset("pool", Sst[:, :], 0.0)
    memset("pool", Sbf[:, :], 0.0)

    def asel(out, in_, cmp, base, cm, pat):
        o, i = out.ap, in_.ap
        S.op("pool", lambda e: e.affine_select(out=o, in_=i, pattern=pat, compare_op=cmp, fill=0.0, base=base,
                                               channel_multiplier=cm), outs=[out], ins=[in_])

    asel(identf[:, :], onesf[:, :], ALU.is_equal, 0, 1, [[-1, 128]])
    asel(mle[:, :], onesf[:, :], ALU.is_ge, 0, -1, [[1, 128]])
    asel(mge[:, :], onesf[:, :], ALU.is_ge, 0, 1, [[-1, 128]])
    cp("pool", identb[:, :], identf[:, :])
    memset("pool", negm[:, :], -30000.0)
    asel(negm[:, :], negm[:, :], ALU.is_gt, 0, 1, [[-1, 128]])
    act(abc[:, :], abc[:, :], AF.Exp)
    ts("dve", abc[:, :], abc[:, :], -1.0, None, ALU.mult)
    for h in range(8):
        wt = io[h % 2]
        S.dma("sp", f"io{h % 2}", wt[:, 0:128], dv(W["gm_w_s"][0, h], "w_in_dram"))
        b = bank()
        tr(b[:, 0:128], wt[:, 0:128], identf[:, :])
        tt("dve", WsT[:, h, :], b[:, 0:128], mle[:, :], ALU.mult)
        tt("pool", wt[:, 128:256], wt[:, 0:128], mge[:, :], ALU.mult)
        o_, i_ = rw[:, h:h + 1], wt[:, 128:256]
        S.op("dve", (lambda o_=o_, i_=i_: lambda e: e.reduce_sum(out=o_.ap, in_=i_.ap, axis=AX.X))(), outs=[o_], ins=[i_])
    for h in range(8):
        ts("dve", CstT[:, h * 128:(h + 1) * 128], y2[:, h * 128:(h + 1) * 128], rw[:, h:h + 1], bsT[:, h:h + 1], ALU.mult, ALU.add)

    def conv_dma(grp, out_ap, in_ap, outkey):
        S.dma("pool", grp, dv(out_ap, outkey), dv(in_ap, "w_in_dram"), group=True)

    def conv_ffn(f):
        grp = f"cv_f{f}"
        wg, wu, wd = W[f"ffn{f}_w_gate"][0], W[f"ffn{f}_w_up"][0], W[f"ffn{f}_w_down"][0]
        wgv = wg.rearrange("(kc p) f -> p kc f", p=128)
        wuv = wu.rearrange("(kc p) f -> p kc f", p=128)
        for fc in range(KF):
            conv_dma(grp, sc[f"g{f}"][fc].rearrange("p (kc f) -> p kc f", kc=KD), wgv[:, :, fc * 128:(fc + 1) * 128], (f"g{f}", fc))
            conv_dma(grp, sc[f"u{f}"][fc].rearrange("p (kc f) -> p kc f", kc=KD), wuv[:, :, fc * 128:(fc + 1) * 128], (f"u{f}", fc))
        wdv = wd.rearrange("(fc p) d -> p fc d", p=128)
        for dc in range(KD):
            conv_dma(grp, sc[f"d{f}"][dc].rearrange("p (fc d) -> p fc d", fc=KF), wdv[:, :, dc * 128:(dc + 1) * 128], (f"d{f}", dc))

    conv_ffn(1)
    winv = W["w_in"][0].rearrange("(kc p) f -> p kc f", p=128)
    for j in range(12):
        conv_dma("cv_in", sc["xbc"][j].rearrange("p (kc f) -> p kc f", kc=KD), winv[:, :, 3072 + j * 128:3072 + (j + 1) * 128], ("xbc", j))
    for i in range(6):
        conv_dma("cv_in", sc["uvz"][i].rearrange("p (kc f) -> p kc f", kc=KD), winv[:, :, i * 512:(i + 1) * 512], ("uvz", i))
    conv_dma("cv_in", sc["dt"].rearrange("p (kc f) -> p kc f", kc=KD), winv[:, :, 4608:4624], ("dt", 0))
    wov = W["w_out"][0].rearrange("(kc p) d -> p kc d", p=128)
    wosc = sc["wo"].rearrange("dc p (kc d) -> p kc dc d", kc=16)
    for kc in range(16):
        wt = io[kc % 2]
        S.dma("sp", f"io{kc % 2}", wt[:, :], dv(wov[:, kc, :], "w_in_dram"))
        gain = vecT[:, 6, kc:kc + 1] if kc < 8 else vecT[:, 7, kc - 8:kc - 7]
        src = yan[0] if kc % 2 == 0 else ybn[0]
        ts("dve", src[:, :], wt[:, :], gain, None, ALU.mult)
        S.dma("sp", f"cv_wo{kc % 2}", dv(wosc[:, kc], ("wo_part", kc)), src[:, :].re("p (dc d) -> p dc d", dc=KD))
    def conv_late():
        conv_ffn(2)
        wpgv = W["ple_w_gate"][0].rearrange("(kc p) f -> p kc f", p=128)
        for dc in range(KD):
            conv_dma("cv_ple", sc["pg"][dc].rearrange("p (kc f) -> p kc f", kc=KD), wpgv[:, :, dc * 128:(dc + 1) * 128], ("pg", dc))
        wppv = W["ple_w_proj"][0].rearrange("(kc p) f -> p kc f", p=128)
        for j in range(KD):
            conv_dma("cv_ple", sc["pp"][j].rearrange("p (kc f) -> p kc f", kc=2), wppv[:, :, j * 128:(j + 1) * 128], ("pp", j))

    sm_uses, bg_uses = [], []
    def seq_for_tile():
        sm_, bg_ = [], []
        for fc in range(KF):
            sm_.append(("g1", fc)); sm_.append(("u1", fc))
        for dc in range(KD):
            bg_.append(("d1", dc))
        for j in range(12):
            sm_.append(("xbc", j))
        for i in range(6):
            bg_.append(("uvz", i))
        for dc in range(KD):
            bg_.append(("wo", dc))
        for fc in range(KF):
            sm_.append(("g2", fc)); sm_.append(("u2", fc))
        for dc in range(KD):
            bg_.append(("d2", dc))
        for dc in range(KD):
            sm_.append(("pg", dc))
            sm_.append(("pp", dc))
        return sm_, bg_
    for n in range(nt):
        a, b = seq_for_tile()
        sm_uses += a
        bg_uses += b
    WSZ = {"g1": 1024, "u1": 1024, "g2": 1024, "u2": 1024, "xbc": 1024, "pg": 1024, "pp": 256,
           "d1": KF * 128, "d2": KF * 128, "uvz": 4096, "wo": 2048}

    class Ring:
        def __init__(self, name, slots, uses):
            self.name, self.slots, self.uses = name, slots, uses
            self.next_fetch = 0
            self.next_take = 0

        def fetch(self):
            i = self.next_fetch
            if i >= len(self.uses):
                return
            kind, idx = self.uses[i]
            slot = self.slots[i % len(self.slots)]
            n_ = WSZ[kind]
            keys = [("wo_part", k) for k in range(16)] if kind == "wo" else [(kind, idx)]
            S.dma("sp", f"{self.name}{i % len(self.slots)}", slot[:, 0:n_], V(sc[kind][idx], keys))
            self.next_fetch += 1

        def take(self, kind, idx):
            i = self.next_take
            while self.uses[i] != (kind, idx):
                assert dbg, (self.uses[i], kind, idx)
                i += 1
                self.next_fetch = max(self.next_fetch, i)
            self.next_take = i
            self.next_take += 1
            while self.next_fetch < min(len(self.uses), i + len(self.slots) - 1):
                self.fetch()
            return self.slots[i % len(self.slots)]

    smR = Ring("smr", smr, sm_uses)
    bgR = Ring("bgr", bgr, bg_uses)

    def rmsnorm_T(kind):
        b = bank()
        for kc in range(KD):
            act(sq[kc % 2][:, :], hT.v((slice(None), kc, slice(None)), [("hT", kc)]), AF.Square)
            mm(b[:, :], onesb[:, :], sq[kc % 2][:, :], start=(kc == 0), stop=(kc == KD - 1))
        act(srt[:, :], b[:, :], AF.Ln, bias=epsb[:, 0:1], scale=1.0 / D)
        act(rstd[:, :], srt[:, :], AF.Exp, scale=-0.5)

    def hTv(kc, cols=slice(None)):
        return hT.v((slice(None), kc, cols), [("hT", kc)])

    def nTv(kc, cols=slice(None)):
        return nT.v((slice(None), kc, cols), [("nT", kc)])

    def Gv(i, cols=slice(None)):
        return G.v((slice(None), i, cols), [("G", i)])

    epsb = sb("epsb", [128, 2], F32)
    memset("pool", epsb[:, 0:1], EPS)
    memset("pool", epsb[:, 1:2], 1.0)

    def norm_to_nT(kind):
        rmsnorm_T(kind)
        for kc in range(KD):
            stt(nTv(kc), hTv(kc), vecT[:, kind, kc:kc + 1], rstd[:, :], ALU.mult, ALU.mult)

    def ffn(f, kind):
        norm_to_nT(kind)
        for fc in range(KF):
            wg = smR.take(f"g{f}", fc)
            wu = smR.take(f"u{f}", fc)
            bg_ = bank()
            for kc in range(KD):
                mm(bg_[:, :], wg[:, kc * 128:(kc + 1) * 128], nTv(kc), start=(kc == 0), stop=(kc == KD - 1))
            bu_ = bank()
            for kc in range(KD):
                mm(bu_[:, :], wu[:, kc * 128:(kc + 1) * 128], nTv(kc), start=(kc == 0), stop=(kc == KD - 1))
            act(sgt[fc % 2][:, :], bg_[:, :], AF.Silu)
            tt("dve", Gv(fc), bu_[:, :], sgt[fc % 2][:, :], ALU.mult)
        for dc in range(KD):
            wd = bgR.take(f"d{f}", dc)
            bd = bank()
            for fc in range(KF):
                mm(bd[:, :], wd[:, fc * 128:(fc + 1) * 128], Gv(fc), start=(fc == 0), stop=(fc == KF - 1))
            stt(hTv(dc), bd[:, :], 0.5, hTv(dc), ALU.mult, ALU.add)

    def load_x(n):
        for c in range(NCHK):
            r0 = n * T + c * 128
            xin = io[c % 2]
            S.dma("sp", f"io{c % 2}", xin[:, :], dv(x_d[r0:r0 + 128, :], "x_dram"))
            cols = slice(c * 128, (c + 1) * 128)
            for half in range(2):
                b = bank()
                for j in range(4):
                    kc = half * 4 + j
                    tr(b[:, j * 128:(j + 1) * 128], xin[:, kc * 128:(kc + 1) * 128], identf[:, :])
                keys = [("hT", half * 4 + j) for j in range(4)]
                cp("act" if half == 0 else "dve", hT.v((slice(None), slice(half * 4, half * 4 + 4), cols), keys),
                   b[:, :].re("p (j t) -> p j t", j=4))

    def load_p(n):
        for c in range(NCHK):
            r0 = n * T + c * 128
            pi = pin[c % 2]
            S.dma("sp", f"pin{c % 2}", pi[:, :], dv(p_d[r0:r0 + 128, :], "p_dram"))
            b = bank()
            for kc in range(2):
                tr(b[:, kc * 128:(kc + 1) * 128], pi[:, kc * 128:(kc + 1) * 128], identf[:, :])
            cp("act", pT[:, :, c * 128:(c + 1) * 128], b[:, 0:256].re("p (j t) -> p j t", j=2))

    def mixer(n):
        norm_to_nT(1)
        for j in range(12):
            w = smR.take("xbc", j)
            b = bank()
            for kc in range(KD):
                mm(b[:, :], w[:, kc * 128:(kc + 1) * 128], nTv(kc), start=(kc == 0), stop=(kc == KD - 1))
            pr = pre[j % 2]
            cp("pool", pr[:, 0:3], halo[:, j, :])
            cp("act", pr[:, 3:T + 3], b[:, :])
            cp("pool", halo[:, j, :], pr[:, T:T + 3])
            ac = cacc[j % 2]
            ts("dve", ac[:, :], pr[:, 0:T], cw[:, j, 0:1], cb[:, j:j + 1], ALU.mult, ALU.add)
            for k in range(1, 4):
                stt(ac[:, :], pr[:, k:k + T], cw[:, j, k:k + 1], ac[:, :], ALU.mult, ALU.add)
            act(xbcT[:, j, :], ac[:, :], AF.Silu)
        for c in range(NCHK):
            b = bank()
            for kc in range(KD):
                mm(b[:, 0:16], nTv(kc, slice(c * 128, (c + 1) * 128)), wdt[:, kc * 16:(kc + 1) * 16],
                   start=(kc == 0), stop=(kc == KD - 1))
            tt("dve", sm["t16a"][:, c, :], b[:, 0:16], dtb[:, :], ALU.add)
        act(sm["t16b"][:, :, :], sm["t16a"][:, :, :], AF.Exp)
        act(sm["dt"][:, :, :], sm["t16b"][:, :, :], AF.Ln, bias=epsb[:, 1:2], scale=1.0)
        tt("dve", sm["adt"][:, :, :], sm["dt"][:, :, :], abc[:, :].re("p (o h) -> p o h", o=1).bcast([128, NCHK, 16]), ALU.mult)
        bq = bank()
        mm(bq[:, 0:64], mle[:, :], sm["adt"][:, :, :].re("p c h -> p (c h)"))
        mm(bq[:, 64:128], onesf[:, :], sm["adt"][:, :, :].re("p c h -> p (c h)"))
        act(sm["nacs"][:, :, :].re("p c h -> p (c h)"), bq[:, 0:64], AF.Copy, scale=-1.0)
        act(sm["eacs"][:, :, :].re("p c h -> p (c h)"), bq[:, 0:64], AF.Exp)
        act(sm["etot"][:, :, :].re("p c h -> p (c h)"), bq[:, 64:128], AF.Exp)
        tt("dve", sm["t16c"][:, :, :].re("p c h -> p (c h)"), bq[:, 64:128], sm["nacs"][:, :, :].re("p c h -> p (c h)"), ALU.add)
        act(sm["ds"][:, :, :], sm["t16c"][:, :, :], AF.Exp)
        tt("dve", sm["dtds"][:, :, :], sm["dt"][:, :, :], sm["ds"][:, :, :], ALU.mult)

        def proj(w, c, i_half, evac):
            b = bank()
            for kc in range(KD):
                mm(b[:, :], nTv(kc, slice(c * 128, (c + 1) * 128)), w[:, kc * 512:(kc + 1) * 512],
                   start=(kc == 0), stop=(kc == KD - 1))
            evac(b, c, i_half)

        def ev_u(b, c, ih):
            act(Gv(8 + 2 * c + ih), b[:, :], AF.Gelu)

        def ev_z(b, c, ih):
            act(zB[:, c, ih * 512:(ih + 1) * 512], b[:, :], AF.Silu)

        def ev_v(b, c, ih):
            act(v_g[c % 2][:, ih * 512:(ih + 1) * 512], b[:, :], AF.Gelu)
            o_, i_ = st6[:, ih, :], v_g[c % 2][:, ih * 512:(ih + 1) * 512]
            S.op("dve", lambda e: e.bn_stats(out=o_.ap, in_=i_.ap), outs=[o_], ins=[i_])

        for ih in range(2):
            w = bgR.take("uvz", ih)
            for c in range(NCHK):
                proj(w, c, ih, ev_u)
        wv0 = bgR.take("uvz", 2)
        wv1 = bgR.take("uvz", 3)
        for c in range(NCHK):
            proj(wv0, c, 0, ev_v)
            proj(wv1, c, 1, ev_v)
            o_, i_ = mv[:, :], st6[:, :, :].re("p a b -> p (a b)")
            S.op("dve", (lambda o_=o_, i_=i_: lambda e: e.bn_aggr(out=o_.ap, in_=i_.ap))(), outs=[o_], ins=[i_])
            act(s1["lnr"][:, 0:1], mv[:, 1:2], AF.Ln, bias=epsb[:, 0:1], scale=1.0)
            act(s1["lnr2"][:, 0:1], s1["lnr"][:, 0:1], AF.Exp, scale=-0.5)
            stt(s1["lnb"][:, 0:1], mv[:, 0:1], -1.0, s1["lnr2"][:, 0:1], ALU.mult, ALU.mult)
            for ih in range(2):
                act(Gv(16 + 2 * c + ih), v_g[c % 2][:, ih * 512:(ih + 1) * 512], AF.Identity,
                    bias=s1["lnb"][:, 0:1], scale=s1["lnr2"][:, 0:1])
        for ih in range(2):
            w = bgR.take("uvz", 4 + ih)
            for c in range(NCHK):
                proj(w, c, ih, ev_z)
        def stage_a(c):
            q = c % 2
            cols = slice(c * 128, (c + 1) * 128)
            bt2 = bank()
            bt2b = bt2[:, :].bitcast(BF16)
            for j in range(8):
                tr(bt2b[:, j * 128:(j + 1) * 128], xbcT[:, j, cols], identb[:, :])
            bt3 = bank()
            bt3b = bt3[:, :].bitcast(BF16)
            for g in range(2):
                tr(bt3b[:, g * 128:(g + 1) * 128], xbcT[:, 8 + g, cols], identb[:, :])
            bc_ = bank()
            for g in range(2):
                mm(bc_[:, g * 128:(g + 1) * 128], xbcT[:, 8 + g, cols], xbcT[:, 10 + g, cols])
            cp("act", xs_tok[q][:, :], bt2b[:, :])
            cp("act", Btok[q][:, :], bt3b[:, 0:256])
            for g in range(2):
                tt("dve", cbm[q][:, g, :], bc_[:, g * 128:(g + 1) * 128], mle[:, :], ALU.mult)
            xs3 = xs_tok[q][:, :].re("p (h q) -> p h q", h=NH)
            tt("dve", xdt[q][:, :].re("p (h q) -> p h q", h=NH), xs3,
               sm["dt"][:, c, :].re("p (h o) -> p h o", o=1).bcast([128, NH, HP]), ALU.mult)
            tt("pool", xdtds[q][:, :].re("p (h q) -> p h q", h=NH), xs3,
               sm["dtds"][:, c, :].re("p (h o) -> p h o", o=1).bcast([128, NH, HP]), ALU.mult)
            bks = []
            for hq in range(4):
                bk = bank()
                bks.append(bk)
                for j in range(4):
                    h = hq * 4 + j
                    rb = Rb[h % 4]
                    ts("pool", rb[:, :], mle[:, :], sm["adt"][:, c, h:h + 1], 0.0, ALU.mult, ALU.add)
                    mm(bk[:, j * 128:(j + 1) * 128], onesf[:, :], rb[:, :], start=True, stop=False)
                    mm(bk[:, j * 128:(j + 1) * 128], identb[:, :], negm[:, :], start=False, stop=True)
            bs_ = [bank(), bank()]
            for h in range(8):
                vh = Gv(16 + 2 * c + h // 4, slice((h % 4) * 128, (h % 4 + 1) * 128))
                mm(bs_[h // 4][:, (h % 4) * 128:(h % 4 + 1) * 128], WsT[:, h, :], vh)
            for hq in range(4):
                for j in range(4):
                    h = hq * 4 + j
                    act(decT[q][:, h, :], bks[hq][:, j * 128:(j + 1) * 128], AF.Exp, bias=sm["nacs"][:, c, h:h + 1], scale=1.0)
            for g in range(2):
                tt("dve", decT[q][:, 8 * g:8 * g + 8, :], decT[q][:, 8 * g:8 * g + 8, :],
                   cbm[q][:, g:g + 1, :].bcast([128, 8, 128]), ALU.mult)
            for ih in range(2):
                fs = slice(ih * 512, (ih + 1) * 512)
                tt("dve", mx[q][:, fs], bs_[ih][:, :], glnb[:, fs], ALU.mult)
                tt("pool", mx[q][:, fs], mx[q][:, fs], CstT[:, fs], ALU.add)
                tt("pool", mx[q][:, fs], mx[q][:, fs], Gv(8 + 2 * c + ih), ALU.mult)
            act(yan[q][:, :], mx[q][:, :], AF.Square, accum=s1["ssa"][:, q:q + 1])
            act(s1["ra"][:, q:q + 1], s1["ssa"][:, q:q + 1], AF.Ln, bias=epsb[:, 0:1], scale=1.0 / 1024)
            act(s1["ra2"][:, q:q + 1], s1["ra"][:, q:q + 1], AF.Exp, scale=-0.5)
            act(yan[q][:, :], mx[q][:, :], AF.Copy, scale=s1["ra2"][:, q:q + 1])
            bt = bank()
            btb = bt[:, :].bitcast(BF16)
            for kc in range(8):
                tr(btb[:, kc * 128:(kc + 1) * 128], yan[q][:, kc * 128:(kc + 1) * 128], identb[:, :])
            cp("dve", nT.v((slice(None), slice(0, 8), cols), [("nT", k) for k in range(8)]),
               btb[:, :].re("p (j t) -> p j t", j=8))

        def stage_b(c):
            q = c % 2
            cols = slice(c * 128, (c + 1) * 128)
            by = [bank(), bank()]
            for h in range(NH):
                mm(by[h // 8][:, (h % 8) * 64:(h % 8 + 1) * 64], decT[q][:, h, :], xdt[q][:, h * 64:(h + 1) * 64])
            bo = [bank(), bank()]
            for g in range(2):
                mm(bo[g][:, :], xbcT[:, 10 + g, cols], Sbf[:, g * 512:(g + 1) * 512])
            bst = [bank(), bank()]
            for g in range(2):
                mm(bst[g][:, :], Btok[q][:, g * 128:(g + 1) * 128], xdtds[q][:, g * 512:(g + 1) * 512])
            for g in range(2):
                fs = slice(g * 512, (g + 1) * 512)
                tt("pool", Sst[:, fs].re("p (h q) -> p h q", h=8), Sst[:, fs].re("p (h q) -> p h q", h=8),
                   sm["etot"][:, c, 8 * g:8 * g + 8].re("p (h o) -> p h o", o=1).bcast([128, 8, HP]), ALU.mult)
            for g in range(2):
                fs = slice(g * 512, (g + 1) * 512)
                tt("dve", y1[q][:, fs].re("p (h q) -> p h q", h=8), bo[g][:, :].re("p (h q) -> p h q", h=8),
                   sm["eacs"][:, c, 8 * g:8 * g + 8].re("p (h o) -> p h o", o=1).bcast([128, 8, HP]), ALU.mult)
                tt("dve", Sst[:, fs], bst[g][:, :], Sst[:, fs], ALU.add)
                cp("act", Sbf[:, fs], Sst[:, fs])
            for g in range(2):
                fs = slice(g * 512, (g + 1) * 512)
                tt("dve", y1[q][:, fs], by[g][:, :], y1[q][:, fs], ALU.add)
                tt("pool", y2[:, fs].re("p (h q) -> p h q", h=8), xs_tok[q][:, fs].re("p (h q) -> p h q", h=8),
                   dsk[:, 8 * g:8 * g + 8].re("p (h o) -> p h o", o=1).bcast([128, 8, HP]), ALU.mult)
                tt("pool", y2[:, fs], y2[:, fs], y1[q][:, fs], ALU.add)
                tt("pool", y2[:, fs], y2[:, fs], zB[:, c, fs], ALU.mult)
                act(ybn[q][:, fs], y2[:, fs], AF.Square, accum=s1["ssb"][:, 2 * q + g:2 * q + g + 1])
            act(s1["rb"][:, 2 * q:2 * q + 2], s1["ssb"][:, 2 * q:2 * q + 2], AF.Ln, bias=epsb[:, 0:1], scale=1.0 / 512)
            act(s1["rb2"][:, 2 * q:2 * q + 2], s1["rb"][:, 2 * q:2 * q + 2], AF.Exp, scale=-0.5)
            for g in range(2):
                fs = slice(g * 512, (g + 1) * 512)
                act(ybn[q][:, fs], y2[:, fs], AF.Copy, scale=s1["rb2"][:, 2 * q + g:2 * q + g + 1])
            bt4 = bank()
            bt4b = bt4[:, :].bitcast(BF16)
            for kc in range(8):
                tr(bt4b[:, kc * 128:(kc + 1) * 128], ybn[q][:, kc * 128:(kc + 1) * 128], identb[:, :])
            cp("act", G.v((slice(None), slice(0, 8), cols), [("G", k) for k in range(8)]),
               bt4b[:, :].re("p (j t) -> p j t", j=8))

        stage_a(0)
        for c in range(NCHK):
            if c + 1 < NCHK:
                stage_a(c + 1)
            stage_b(c)
        for dc in range(KD):
            w = bgR.take("wo", dc)
            b = bank()
            for kc in range(16):
                rhs = nTv(kc) if kc < 8 else Gv(kc - 8)
                mm(b[:, :], w[:, kc * 128:(kc + 1) * 128], rhs, start=(kc == 0), stop=(kc == 15))
            tt("dve", hTv(dc), b[:, :], hTv(dc), ALU.add)


    def ple(n):
        load_p(n)
        norm_to_nT(3)
        wp = None
        for dc in range(KD):
            w = smR.take("pg", dc)
            wp = smR.take("pp", dc)
            b = bank()
            for kc in range(KD):
                mm(b[:, :], w[:, kc * 128:(kc + 1) * 128], nTv(kc), start=(kc == 0), stop=(kc == KD - 1))
            g_ = cacc[dc % 2]
            act(g_[:, :], b[:, :], AF.Sigmoid, bias=vecT[:, 5, dc:dc + 1], scale=1.0)
            b2 = bank()
            for kc in range(2):
                mm(b2[:, :], wp[:, kc * 128:(kc + 1) * 128], pT[:, kc, :], start=(kc == 0), stop=(kc == 1))
            tt("dve", g_[:, :], b2[:, :], g_[:, :], ALU.mult)
            tt("pool", hTv(dc), hTv(dc), g_[:, :], ALU.add)

    def final_and_store(n, do_norm=True):
        if do_norm:
            rmsnorm_T(4)
            for kc in range(KD):
                stt(hTv(kc), hTv(kc), vecT[:, 4, kc:kc + 1], rstd[:, :], ALU.mult, ALU.mult)
        for c in range(NCHK):
            cols = slice(c * 128, (c + 1) * 128)
            ot = io[c % 2]
            for half in range(2):
                b = bank()
                for j in range(4):
                    kc = half * 4 + j
                    tr(b[:, j * 128:(j + 1) * 128], hTv(kc, cols), identf[:, :])
                cp("act" if half == 0 else "dve", ot[:, half * 512:(half + 1) * 512], b[:, :])
            r0 = n * T + c * 128
            S.dma("sp", f"io{c % 2}", dv(out_d[r0:r0 + 128, :], ("out", n, c)), ot[:, :])

    conv_late()
    for n in range(nt):
        load_x(n)
        if dbg == "x":
            final_and_store(n, False); continue
        ffn(1, 0)
        if dbg == "ffn1":
            final_and_store(n, False); continue
        if n == 0:
            S.dma("sp", "wdt", wdt[:, :], dv(sc["dt"], ("dt", 0)))
        mixer(n)
        if dbg == "mix":
            final_and_store(n, False); continue
        ffn(2, 2)
        if dbg == "ffn2":
            final_and_store(n, False); continue
        ple(n)
        if dbg == "ple":
            final_and_store(n, False); continue
        final_and_store(n)
    S.final_wait_all("sp")

    sems = {e: es.enter_context(nc.semaphore("s_" + e)) for e in ENGS}
    dma_sems = {k: es.enter_context(nc.semaphore("d_" + k)) for k in S.dma_cnt}
    block = es.enter_context(nc.Block())
    S.emit_all(nc, block, sems, dma_sems)
    es.close()
    return nc


_NC_CACHE = {}


def kernel(**inputs):
    nt = SEQ // T
    if nt not in _NC_CACHE:
        _NC_CACHE[nt] = build_nc(nt)
    nc = _NC_CACHE[nt]
    x = np.ascontiguousarray(inputs["x"], dtype=np.float32)
    p = np.ascontiguousarray(inputs["p"], dtype=np.float32)
    wmap = {n: np.ascontiguousarray(inputs[n], dtype=np.float32) for n in WNAMES}
    in_maps = []
    for c in range(N_CORES):
        m = {"x": x[c], "p": p[0, c]}
        m.update(wmap)
        in_maps.append(m)
    res = run_bass_kernel_spmd(nc, in_maps, core_ids=list(range(N_CORES)))
    return np.stack([np.asarray(r["out"], dtype=np.float32) for r in res.results], axis=0)
```

```python
import numpy as np
from contextlib import ExitStack
import concourse.bass as bass
import concourse.mybir as mybir
from concourse.bass_utils import run_bass_kernel_spmd

F32 = mybir.dt.float32
BF16 = mybir.dt.bfloat16
AF = mybir.ActivationFunctionType
ALU = mybir.AluOpType
AX = mybir.AxisListType

D = 1024
KD = 8
DFF = 2816
KF = 22
T = 512
NCHK = 4
SEQ = 8192
NH = 16
HP = 64
DIN = 4624
EPS = 1e-6
N_CORES = 8


class V:
    __slots__ = ("ap", "k")

    def __init__(self, ap, k):
        self.ap = ap
        self.k = tuple(k)

    def __getitem__(self, idx):
        return V(self.ap[idx], self.k)

    def re(self, pat, **kw):
        return V(self.ap.rearrange(pat, **kw), self.k)

    def bcast(self, shape):
        return V(self.ap.broadcast_to(shape), self.k)

    def bitcast(self, dt):
        return V(self.ap.bitcast(dt), self.k)

    def keys(self, k):
        return V(self.ap, k)


class Tl:
    def __init__(self, handle, key):
        self.t = handle
        self.key = key

    def __getitem__(self, idx):
        return V(self.t[idx], (self.key,))

    def v(self, idx, keys):
        return V(self.t[idx], keys)


ENGS = ("pe", "act", "dve", "pool", "sp")


class Sched:
    def __init__(self):
        self.ops = {e: [] for e in ENGS}
        self.last_w = {}
        self.readers = {}
        self.dma_cnt = {}
        self.dma_group = set()

    def _deps(self, eng, reads, writes):
        deps = set()
        for b in reads:
            if b in self.last_w:
                d = self.last_w[b]
                deps.add(d)
        for b in writes:
            if b in self.last_w:
                d = self.last_w[b]
                deps.add(d)
            for d in self.readers.get(b, ()):
                deps.add(d)
        out = []
        for d in deps:
            if d[0] == "dma":
                out.append(d)
            else:
                e2, i2 = d
                if e2 == eng and eng in ("pe", "sp"):
                    continue
                self.ops[e2][i2]["sig"] = True
                out.append(d)
        return out

    def op(self, eng, emit, outs=(), ins=()):
        reads = [k for v in ins for k in v.k]
        writes = [k for v in outs for k in v.k]
        deps = self._deps(eng, reads, writes)
        idx = len(self.ops[eng])
        self.ops[eng].append(dict(emit=emit, deps=deps, sig=False, dma=None))
        tok = (eng, idx)
        for b in writes:
            self.last_w[b] = tok
            self.readers[b] = []
        for b in reads:
            if b not in writes:
                self.readers.setdefault(b, []).append(tok)
        return tok

    def dma(self, queue, semkey, out, in_, group=False, **kw):
        reads = list(in_.k)
        writes = list(out.k)
        deps = self._deps(queue, reads, writes)
        if group:
            deps = [d for d in deps if not (d[0] == "dma" and d[1] == semkey)]
        cnt = self.dma_cnt.get(semkey, 0) + 1
        self.dma_cnt[semkey] = cnt
        if group:
            self.dma_group.add(semkey)
        o_ap, i_ap = out.ap, in_.ap
        idx = len(self.ops[queue])
        self.ops[queue].append(dict(emit=lambda e: e.dma_start(out=o_ap, in_=i_ap, **kw), deps=deps, sig=False, dma=semkey))
        tok = ("dma", semkey, cnt)
        for b in writes:
            self.last_w[b] = tok
            self.readers[b] = []
        for b in reads:
            self.readers.setdefault(b, []).append(tok)
        return tok

    def final_wait_all(self, eng):
        deps = []
        for semkey, cnt in self.dma_cnt.items():
            deps.append(("dma", semkey, cnt))
        self.ops[eng].append(dict(emit=None, deps=deps, sig=False, dma=None))

    def emit_all(self, nc, block, sems, dma_sems):
        signum = {}
        for e in ENGS:
            n = 0
            m = {}
            for i, o in enumerate(self.ops[e]):
                if o["sig"]:
                    n += 1
                    m[i] = n
            signum[e] = m

        def run(eng_name, eng):
            waited = {}
            for o in self.ops[eng_name]:
                need = {}
                for d in o["deps"]:
                    if d[0] == "dma":
                        _, semkey, cnt = d
                        if semkey in self.dma_group:
                            cnt = self.dma_cnt[semkey]
                        key = ("dma", semkey)
                        val = 16 * cnt
                        sem = dma_sems[semkey]
                    else:
                        e2, i2 = d
                        key = ("eng", e2)
                        val = signum[e2][i2]
                        sem = sems[e2]
                    if waited.get(key, 0) >= val:
                        continue
                    if key not in need or need[key][1] < val:
                        need[key] = (sem, val)
                for key, (sem, val) in need.items():
                    eng.wait_ge(sem, val)
                    waited[key] = val
                if o["emit"] is None:
                    continue
                ins = o["emit"](eng)
                if o["dma"] is not None:
                    ins.then_inc(dma_sems[o["dma"]], 16)
                elif o["sig"]:
                    ins.then_inc(sems[eng_name], 1)

        @block.tensor
        def _(e):
            run("pe", e)

        @block.scalar
        def _(e):
            run("act", e)

        @block.vector
        def _(e):
            run("dve", e)

        @block.gpsimd
        def _(e):
            run("pool", e)

        @block.sync
        def _(e):
            run("sp", e)


WNAMES = ["ffn1_norm", "ffn1_w_gate", "ffn1_w_up", "ffn1_w_down", "mix_norm", "w_in", "gm_ln_g", "gm_ln_b",
          "gm_w_s", "gm_b_s", "gm_out_norm", "conv_w", "conv_b", "dt_bias", "a_log", "d_skip", "ssm_norm",
          "w_out", "ffn2_norm", "ffn2_w_gate", "ffn2_w_up", "ffn2_w_down", "ple_norm", "ple_w_gate",
          "ple_b_gate", "ple_w_proj", "final_norm"]
WSHAPES = {
    "ffn1_norm": [1, 1024], "ffn1_w_gate": [1, 1024, 2816], "ffn1_w_up": [1, 1024, 2816], "ffn1_w_down": [1, 2816, 1024],
    "mix_norm": [1, 1024], "w_in": [1, 1024, 4624], "gm_ln_g": [1, 1024], "gm_ln_b": [1, 1024],
    "gm_w_s": [1, 8, 128, 128], "gm_b_s": [1, 8, 128], "gm_out_norm": [1, 1024], "conv_w": [1, 4, 1536],
    "conv_b": [1, 1536], "dt_bias": [1, 16], "a_log": [1, 16], "d_skip": [1, 16], "ssm_norm": [1, 1024],
    "w_out": [1, 2048, 1024], "ffn2_norm": [1, 1024], "ffn2_w_gate": [1, 1024, 2816], "ffn2_w_up": [1, 1024, 2816],
    "ffn2_w_down": [1, 2816, 1024], "ple_norm": [1, 1024], "ple_w_gate": [1, 1024, 1024], "ple_b_gate": [1, 1024],
    "ple_w_proj": [1, 256, 1024], "final_norm": [1024],
}


def build_nc(nt=SEQ // T, dbg=False):
    seq = nt * T
    nc = bass.Bass("TRN2", target_bir_lowering=False)
    S = Sched()
    es = ExitStack()

    def dram(name, shape, dt, kind):
        return nc.dram_tensor(name, shape, dt, kind=kind).ap()

    x_d = dram("x", [seq, D], F32, "ExternalInput")
    p_d = dram("p", [seq, 256], F32, "ExternalInput")
    out_d = dram("out", [seq, D], F32, "ExternalOutput")
    W = {n: dram(n, WSHAPES[n], F32, "ExternalInput") for n in WNAMES}

    sc = {}
    for f in (1, 2):
        sc[f"g{f}"] = dram(f"sc_g{f}", [KF, 128, KD * 128], BF16, "Internal")
        sc[f"u{f}"] = dram(f"sc_u{f}", [KF, 128, KD * 128], BF16, "Internal")
        sc[f"d{f}"] = dram(f"sc_d{f}", [KD, 128, KF * 128], BF16, "Internal")
    sc["uvz"] = dram("sc_uvz", [6, 128, KD * 512], BF16, "Internal")
    sc["xbc"] = dram("sc_xbc", [12, 128, KD * 128], BF16, "Internal")
    sc["dt"] = dram("sc_dt", [128, KD * 16], BF16, "Internal")
    sc["wo"] = dram("sc_wo", [KD, 128, 16 * 128], BF16, "Internal")
    sc["pg"] = dram("sc_pg", [KD, 128, KD * 128], BF16, "Internal")
    sc["pp"] = dram("sc_pp", [KD, 128, 2 * 128], BF16, "Internal")

    def sb(name, shape, dt, key=None):
        h = es.enter_context(nc.sbuf_tensor(name, shape, dt))
        return Tl(h, key or name)

    hT = sb("hT", [128, KD, T], F32)
    nT = sb("nT", [128, KD, T], BF16)
    G = sb("G", [128, 24, T], BF16)
    rstd = sb("rstd", [128, T], F32)
    sgt = [sb(f"sgt{i}", [128, T], BF16) for i in range(2)]
    sq = sgt
    NSM, NBG = 6, 3
    smr = [sb(f"smr{i}", [128, 1024], BF16) for i in range(NSM)]
    bgr = [sb(f"bgr{i}", [128, 4096], BF16) for i in range(NBG)]
    wdt = sb("wdt", [128, KD * 16], BF16)
    xbcT = sb("xbcT", [128, 12, T], BF16)
    pre = [sb(f"pre{i}", [128, T + 3], F32) for i in range(2)]
    halo = sb("halo", [128, 12, 3], F32)
    cacc = [sb(f"cacc{i}", [128, T], F32) for i in range(2)]
    Sst = sb("Sst", [128, 1024], F32)
    Sbf = sb("Sbf", [128, 1024], BF16)
    io = [sb(f"io{i}", [128, 1024], F32) for i in range(2)]
    pin = [sb(f"pin{i}", [128, 256], F32) for i in range(2)]
    pT = sb("pT", [128, 2, T], BF16)
    WsT = sb("WsT", [128, 8, 128], BF16)
    CstT = sb("CstT", [128, 1024], F32)
    glnb = sb("glnb", [128, 1024], F32)
    identb = sb("identb", [128, 128], BF16)
    identf = sb("identf", [128, 128], F32)
    mle = sb("mle", [128, 128], F32)
    mge = sb("mge", [128, 128], F32)
    negm = sb("negm", [128, 128], BF16)
    onesf = sb("onesf", [128, 128], F32)
    onesb = sb("onesb", [128, 128], BF16)
    vecT = sb("vecT", [128, 8, 8], F32)
    cw = sb("cw", [128, 12, 4], F32)
    cb = sb("cb", [128, 12], F32)
    dtb = sb("dtb", [128, 16], F32)
    abc = sb("abc", [128, 16], F32)
    dsk = sb("dsk", [128, 16], F32)
    bsT = sb("bsT", [128, 8], F32)
    rw = sb("rw", [128, 8], F32)
    zB = sb("zB", [128, NCHK, 1024], BF16)
    mx = [sb(f"mx{i}", [128, 1024], F32) for i in range(2)]
    yan = [sb(f"yan{i}", [128, 1024], BF16) for i in range(2)]
    xs_tok = [sb(f"xs_tok{i}", [128, 1024], BF16) for i in range(2)]
    xdt = [sb(f"xdt{i}", [128, 1024], BF16) for i in range(2)]
    xdtds = [sb(f"xdtds{i}", [128, 1024], BF16) for i in range(2)]
    Btok = [sb(f"Btok{i}", [128, 256], BF16) for i in range(2)]
    decT = [sb(f"decT{i}", [128, 16, 128], BF16) for i in range(2)]
    cbm = [sb(f"cbm{i}", [128, 2, 128], F32) for i in range(2)]
    y1 = [sb(f"y1{i}", [128, 1024], F32) for i in range(2)]
    y2 = sb("y2", [128, 1024], F32)
    Rb = [sb(f"Rb{i}", [128, 4, 128], F32) for i in range(2)]
    ybn = [sb(f"ybn{i}", [128, 1024], BF16) for i in range(2)]
    v_g = y1
    st6 = sb("st6", [128, 2, 6], F32)
    mv = sb("mv", [128, 2], F32)
    sm = {n: sb("sm_" + n, [128, NCHK, 16], F32) for n in
          ["t16a", "t16b", "dt", "adt", "nacs", "eacs", "t16c", "ds", "dtds", "etot"]}
    s1 = {n: sb("s1_" + n, [128, 4], F32) for n in ["lnr", "lnr2", "lnb", "ssa", "ra", "ra2", "ssb", "rb", "rb2"]}

    banks = [Tl(es.enter_context(nc.psum_tensor(f"bank{i}", [128, 512], F32)), f"bank{i}") for i in range(8)]
    bank_ctr = [0]

    def bank():
        b = banks[bank_ctr[0] % 8]
        bank_ctr[0] += 1
        assert not (b.key in S.last_w and not S.readers.get(b.key)), f"PSUM {b.key} re-allocated before its reader was emitted"
        return b

    def mm(out, lhsT, rhs, start=True, stop=True):
        o, l, r = out.ap, lhsT.ap, rhs.ap
        S.op("pe", lambda e: e.matmul(o, lhsT=l, rhs=r, start=start, stop=stop), outs=[out], ins=[lhsT, rhs])

    def tr(out, in_, ident):
        o, i, d = out.ap, in_.ap, ident.ap
        S.op("pe", lambda e: e.transpose(o, i, d), outs=[out], ins=[in_, ident])

    def act(out, in_, func, bias=None, scale=None, accum=None, eng="act"):
        o, i = out.ap, in_.ap
        kw = {}
        ins = [in_]
        outs = [out]
        if bias is not None:
            if isinstance(bias, V):
                kw["bias"] = bias.ap
                ins.append(bias)
            else:
                kw["bias"] = bias
        if scale is not None:
            if isinstance(scale, V):
                kw["scale"] = scale.ap
                ins.append(scale)
            else:
                kw["scale"] = scale
        if accum is not None:
            kw["accum_out"] = accum.ap
            outs.append(accum)
        S.op("act", lambda e: e.activation(out=o, in_=i, func=func, **kw), outs=outs, ins=ins)

    def tt(eng, out, in0, in1, op):
        o, a, b = out.ap, in0.ap, in1.ap
        S.op(eng, lambda e: e.tensor_tensor(out=o, in0=a, in1=b, op=op), outs=[out], ins=[in0, in1])

    def ts(eng, out, in0, s1_, s2_, op0, op1=None):
        o, a = out.ap, in0.ap
        ins = [in0]
        a1 = s1_
        a2 = s2_
        if isinstance(s1_, V):
            ins.append(s1_)
            a1 = s1_.ap
        if isinstance(s2_, V):
            ins.append(s2_)
            a2 = s2_.ap
        if op1 is None:
            S.op(eng, lambda e: e.tensor_scalar(out=o, in0=a, scalar1=a1, scalar2=None, op0=op0), outs=[out], ins=ins)
        else:
            S.op(eng, lambda e: e.tensor_scalar(out=o, in0=a, scalar1=a1, scalar2=a2, op0=op0, op1=op1), outs=[out], ins=ins)

    def stt(out, in0, scalar, in1, op0, op1):
        o, a, b = out.ap, in0.ap, in1.ap
        ins = [in0, in1]
        sc_ = scalar
        if isinstance(scalar, V):
            ins.append(scalar)
            sc_ = scalar.ap
        S.op("dve", lambda e: e.scalar_tensor_tensor(out=o, in0=a, scalar=sc_, in1=b, op0=op0, op1=op1), outs=[out], ins=ins)

    def cp(eng, out, in_):
        o, i = out.ap, in_.ap
        if eng == "act":
            S.op("act", lambda e: e.activation(out=o, in_=i, func=AF.Copy), outs=[out], ins=[in_])
        else:
            S.op(eng, lambda e: e.tensor_copy(out=o, in_=i), outs=[out], ins=[in_])

    def recip(out, in_):
        o, i = out.ap, in_.ap
        S.op("dve", lambda e: e.reciprocal(out=o, in_=i), outs=[out], ins=[in_])

    def memset(eng, out, val):
        o = out.ap
        S.op(eng, lambda e: e.memset(o, val), outs=[out])

    def dv(ap, key):
        return V(ap, (key,))

    pro = "pro"

    def pload(out, in_ap, key="w_in_dram"):
        S.dma("sp", pro, out, dv(in_ap, key), group=True, allow_slow_non_contiguous=True)

    for i, n in enumerate(["ffn1_norm", "mix_norm", "ffn2_norm", "ple_norm", "final_norm", "ple_b_gate", "gm_out_norm", "ssm_norm"]):
        src = W[n] if n == "final_norm" else W[n][0]
        pload(vecT[:, i, :], src.rearrange("(kc p) -> p kc", p=128))
    for j in range(12):
        pload(cw[:, j, :], W["conv_w"][0][:, j * 128:(j + 1) * 128].rearrange("k c -> c k"))
    pload(cb[:, :], W["conv_b"][0].rearrange("(j c) -> c j", c=128))
    pload(dtb[:, :], W["dt_bias"][0:1, :].broadcast_to([128, 16]))
    pload(abc[:, :], W["a_log"][0:1, :].broadcast_to([128, 16]))
    pload(dsk[:, :], W["d_skip"][0:1, :].broadcast_to([128, 16]))
    pload(bsT[:, :], W["gm_b_s"][0].rearrange("h t -> t h"))
    pload(glnb[:, :], W["gm_ln_g"][0:1, :].broadcast_to([128, 1024]))
    pload(y2[:, :], W["gm_ln_b"][0:1, :].broadcast_to([128, 1024]))

    memset("pool", onesf[:, :], 1.0)
    memset("pool", onesb[:, :], 1.0)
    memset("pool", halo[:, :, :], 0.0)
    memset("pool", Sst[:, :], 0.0)
    memset("pool", Sbf[:, :], 0.0)

    def asel(out, in_, cmp, base, cm, pat):
        o, i = out.ap, in_.ap
        S.op("pool", lambda e: e.affine_select(out=o, in_=i, pattern=pat, compare_op=cmp, fill=0.0, base=base,
                                               channel_multiplier=cm), outs=[out], ins=[in_])

    asel(identf[:, :], onesf[:, :], ALU.is_equal, 0, 1, [[-1, 128]])
    asel(mle[:, :], onesf[:, :], ALU.is_ge, 0, -1, [[1, 128]])
    asel(mge[:, :], onesf[:, :], ALU.is_ge, 0, 1, [[-1, 128]])
    cp("pool", identb[:, :], identf[:, :])
    memset("pool", negm[:, :], -30000.0)
    asel(negm[:, :], negm[:, :], ALU.is_gt, 0, 1, [[-1, 128]])
    act(abc[:, :], abc[:, :], AF.Exp)
    ts("dve", abc[:, :], abc[:, :], -1.0, None, ALU.mult)
    for h in range(8):
        wt = io[h % 2]
        S.dma("sp", f"io{h % 2}", wt[:, 0:128], dv(W["gm_w_s"][0, h], "w_in_dram"))
        b = bank()
        tr(b[:, 0:128], wt[:, 0:128], identf[:, :])
        tt("dve", WsT[:, h, :], b[:, 0:128], mle[:, :], ALU.mult)
        tt("pool", wt[:, 128:256], wt[:, 0:128], mge[:, :], ALU.mult)
        o_, i_ = rw[:, h:h + 1], wt[:, 128:256]
        S.op("dve", (lambda o_=o_, i_=i_: lambda e: e.reduce_sum(out=o_.ap, in_=i_.ap, axis=AX.X))(), outs=[o_], ins=[i_])
    for h in range(8):
        ts("dve", CstT[:, h * 128:(h + 1) * 128], y2[:, h * 128:(h + 1) * 128], rw[:, h:h + 1], bsT[:, h:h + 1], ALU.mult, ALU.add)

    def conv_dma(grp, out_ap, in_ap, outkey):
        S.dma("pool", grp, dv(out_ap, outkey), dv(in_ap, "w_in_dram"), group=True)

    def conv_ffn(f):
        grp = f"cv_f{f}"
        wg, wu, wd = W[f"ffn{f}_w_gate"][0], W[f"ffn{f}_w_up"][0], W[f"ffn{f}_w_down"][0]
        wgv = wg.rearrange("(kc p) f -> p kc f", p=128)
        wuv = wu.rearrange("(kc p) f -> p kc f", p=128)
        for fc in range(KF):
            conv_dma(grp, sc[f"g{f}"][fc].rearrange("p (kc f) -> p kc f", kc=KD), wgv[:, :, fc * 128:(fc + 1) * 128], (f"g{f}", fc))
            conv_dma(grp, sc[f"u{f}"][fc].rearrange("p (kc f) -> p kc f", kc=KD), wuv[:, :, fc * 128:(fc + 1) * 128], (f"u{f}", fc))
        wdv = wd.rearrange("(fc p) d -> p fc d", p=128)
        for dc in range(KD):
            conv_dma(grp, sc[f"d{f}"][dc].rearrange("p (fc d) -> p fc d", fc=KF), wdv[:, :, dc * 128:(dc + 1) * 128], (f"d{f}", dc))

    conv_ffn(1)
    winv = W["w_in"][0].rearrange("(kc p) f -> p kc f", p=128)
    for j in range(12):
        conv_dma("cv_in", sc["xbc"][j].rearrange("p (kc f) -> p kc f", kc=KD), winv[:, :, 3072 + j * 128:3072 + (j + 1) * 128], ("xbc", j))
    for i in range(6):
        conv_dma("cv_in", sc["uvz"][i].rearrange("p (kc f) -> p kc f", kc=KD), winv[:, :, i * 512:(i + 1) * 512], ("uvz", i))
    conv_dma("cv_in", sc["dt"].rearrange("p (kc f) -> p kc f", kc=KD), winv[:, :, 4608:4624], ("dt", 0))
    wov = W["w_out"][0].rearrange("(kc p) d -> p kc d", p=128)
    wosc = sc["wo"].rearrange("dc p (kc d) -> p kc dc d", kc=16)
    for kc in range(16):
        wt = io[kc % 2]
        S.dma("sp", f"io{kc % 2}", wt[:, :], dv(wov[:, kc, :], "w_in_dram"))
        gain = vecT[:, 6, kc:kc + 1] if kc < 8 else vecT[:, 7, kc - 8:kc - 7]
        src = yan[0] if kc % 2 == 0 else ybn[0]
        ts("dve", src[:, :], wt[:, :], gain, None, ALU.mult)
        S.dma("sp", f"cv_wo{kc % 2}", dv(wosc[:, kc], ("wo_part", kc)), src[:, :].re("p (dc d) -> p dc d", dc=KD))
    def conv_late():
        conv_ffn(2)
        wpgv = W["ple_w_gate"][0].rearrange("(kc p) f -> p kc f", p=128)
        for dc in range(KD):
            conv_dma("cv_ple", sc["pg"][dc].rearrange("p (kc f) -> p kc f", kc=KD), wpgv[:, :, dc * 128:(dc + 1) * 128], ("pg", dc))
        wppv = W["ple_w_proj"][0].rearrange("(kc p) f -> p kc f", p=128)
        for j in range(KD):
            conv_dma("cv_ple", sc["pp"][j].rearrange("p (kc f) -> p kc f", kc=2), wppv[:, :, j * 128:(j + 1) * 128], ("pp", j))

    sm_uses, bg_uses = [], []
    def seq_for_tile():
        sm_, bg_ = [], []
        for fc in range(KF):
            sm_.append(("g1", fc)); sm_.append(("u1", fc))
        for dc in range(KD):
            bg_.append(("d1", dc))
        for j in range(12):
            sm_.append(("xbc", j))
        for i in range(6):
            bg_.append(("uvz", i))
        for dc in range(KD):
            bg_.append(("wo", dc))
        for fc in range(KF):
            sm_.append(("g2", fc)); sm_.append(("u2", fc))
        for dc in range(KD):
            bg_.append(("d2", dc))
        for dc in range(KD):
            sm_.append(("pg", dc))
            sm_.append(("pp", dc))
        return sm_, bg_
    for n in range(nt):
        a, b = seq_for_tile()
        sm_uses += a
        bg_uses += b
    WSZ = {"g1": 1024, "u1": 1024, "g2": 1024, "u2": 1024, "xbc": 1024, "pg": 1024, "pp": 256,
           "d1": KF * 128, "d2": KF * 128, "uvz": 4096, "wo": 2048}

    class Ring:
        def __init__(self, name, slots, uses):
            self.name, self.slots, self.uses = name, slots, uses
            self.next_fetch = 0
            self.next_take = 0

        def fetch(self):
            i = self.next_fetch
            if i >= len(self.uses):
                return
            kind, idx = self.uses[i]
            slot = self.slots[i % len(self.slots)]
            n_ = WSZ[kind]
            keys = [("wo_part", k) for k in range(16)] if kind == "wo" else [(kind, idx)]
            S.dma("sp", f"{self.name}{i % len(self.slots)}", slot[:, 0:n_], V(sc[kind][idx], keys))
            self.next_fetch += 1

        def take(self, kind, idx):
            i = self.next_take
            while self.uses[i] != (kind, idx):
                assert dbg, (self.uses[i], kind, idx)
                i += 1
                self.next_fetch = max(self.next_fetch, i)
            self.next_take = i
            self.next_take += 1
            while self.next_fetch < min(len(self.uses), i + len(self.slots) - 1):
                self.fetch()
            return self.slots[i % len(self.slots)]

    smR = Ring("smr", smr, sm_uses)
    bgR = Ring("bgr", bgr, bg_uses)

    def rmsnorm_T(kind):
        b = bank()
        for kc in range(KD):
            act(sq[kc % 2][:, :], hT.v((slice(None), kc, slice(None)), [("hT", kc)]), AF.Square)
            mm(b[:, :], onesb[:, :], sq[kc % 2][:, :], start=(kc == 0), stop=(kc == KD - 1))
        act(rstd[:, :], b[:, :], AF.Ln, bias=epsb[:, 0:1], scale=1.0 / D)
        act(rstd[:, :], rstd[:, :], AF.Exp, scale=-0.5)

    def hTv(kc, cols=slice(None)):
        return hT.v((slice(None), kc, cols), [("hT", kc)])

    def nTv(kc, cols=slice(None)):
        return nT.v((slice(None), kc, cols), [("nT", kc)])

    def Gv(i, cols=slice(None)):
        return G.v((slice(None), i, cols), [("G", i)])

    epsb = sb("epsb", [128, 2], F32)
    memset("pool", epsb[:, 0:1], EPS)
    memset("pool", epsb[:, 1:2], 1.0)

    def norm_to_nT(kind):
        rmsnorm_T(kind)
        for kc in range(KD):
            stt(nTv(kc), hTv(kc), vecT[:, kind, kc:kc + 1], rstd[:, :], ALU.mult, ALU.mult)

    def ffn(f, kind):
        norm_to_nT(kind)
        for fc in range(KF):
            wg = smR.take(f"g{f}", fc)
            wu = smR.take(f"u{f}", fc)
            bg_ = bank()
            for kc in range(KD):
                mm(bg_[:, :], wg[:, kc * 128:(kc + 1) * 128], nTv(kc), start=(kc == 0), stop=(kc == KD - 1))
            bu_ = bank()
            for kc in range(KD):
                mm(bu_[:, :], wu[:, kc * 128:(kc + 1) * 128], nTv(kc), start=(kc == 0), stop=(kc == KD - 1))
            act(sgt[fc % 2][:, :], bg_[:, :], AF.Silu)
            tt("dve", Gv(fc), bu_[:, :], sgt[fc % 2][:, :], ALU.mult)
        for dc in range(KD):
            wd = bgR.take(f"d{f}", dc)
            bd = bank()
            for fc in range(KF):
                mm(bd[:, :], wd[:, fc * 128:(fc + 1) * 128], Gv(fc), start=(fc == 0), stop=(fc == KF - 1))
            stt(hTv(dc), bd[:, :], 0.5, hTv(dc), ALU.mult, ALU.add)

    def load_x(n):
        for c in range(NCHK):
            r0 = n * T + c * 128
            xin = io[c % 2]
            S.dma("sp", f"io{c % 2}", xin[:, :], dv(x_d[r0:r0 + 128, :], "x_dram"))
            cols = slice(c * 128, (c + 1) * 128)
            for half in range(2):
                b = bank()
                for j in range(4):
                    kc = half * 4 + j
                    tr(b[:, j * 128:(j + 1) * 128], xin[:, kc * 128:(kc + 1) * 128], identf[:, :])
                keys = [("hT", half * 4 + j) for j in range(4)]
                cp("act" if half == 0 else "dve", hT.v((slice(None), slice(half * 4, half * 4 + 4), cols), keys),
                   b[:, :].re("p (j t) -> p j t", j=4))

    def load_p(n):
        for c in range(NCHK):
            r0 = n * T + c * 128
            pi = pin[c % 2]
            S.dma("sp", f"pin{c % 2}", pi[:, :], dv(p_d[r0:r0 + 128, :], "p_dram"))
            b = bank()
            for kc in range(2):
                tr(b[:, kc * 128:(kc + 1) * 128], pi[:, kc * 128:(kc + 1) * 128], identf[:, :])
            cp("act", pT[:, :, c * 128:(c + 1) * 128], b[:, 0:256].re("p (j t) -> p j t", j=2))

    def mixer(n):
        norm_to_nT(1)
        for j in range(12):
            w = smR.take("xbc", j)
            b = bank()
            for kc in range(KD):
                mm(b[:, :], w[:, kc * 128:(kc + 1) * 128], nTv(kc), start=(kc == 0), stop=(kc == KD - 1))
            pr = pre[j % 2]
            cp("pool", pr[:, 0:3], halo[:, j, :])
            cp("act", pr[:, 3:T + 3], b[:, :])
            cp("pool", halo[:, j, :], pr[:, T:T + 3])
            ac = cacc[j % 2]
            ts("dve", ac[:, :], pr[:, 0:T], cw[:, j, 0:1], cb[:, j:j + 1], ALU.mult, ALU.add)
            for k in range(1, 4):
                stt(ac[:, :], pr[:, k:k + T], cw[:, j, k:k + 1], ac[:, :], ALU.mult, ALU.add)
            act(xbcT[:, j, :], ac[:, :], AF.Silu)
        for c in range(NCHK):
            b = bank()
            for kc in range(KD):
                mm(b[:, 0:16], nTv(kc, slice(c * 128, (c + 1) * 128)), wdt[:, kc * 16:(kc + 1) * 16],
                   start=(kc == 0), stop=(kc == KD - 1))
            tt("dve", sm["t16a"][:, c, :], b[:, 0:16], dtb[:, :], ALU.add)
        act(sm["t16b"][:, :, :], sm["t16a"][:, :, :], AF.Exp)
        act(sm["dt"][:, :, :], sm["t16b"][:, :, :], AF.Ln, bias=epsb[:, 1:2], scale=1.0)
        tt("dve", sm["adt"][:, :, :], sm["dt"][:, :, :], abc[:, :].re("p (o h) -> p o h", o=1).bcast([128, NCHK, 16]), ALU.mult)
        bq = bank()
        mm(bq[:, 0:64], mle[:, :], sm["adt"][:, :, :].re("p c h -> p (c h)"))
        mm(bq[:, 64:128], onesf[:, :], sm["adt"][:, :, :].re("p c h -> p (c h)"))
        act(sm["nacs"][:, :, :].re("p c h -> p (c h)"), bq[:, 0:64], AF.Copy, scale=-1.0)
        act(sm["eacs"][:, :, :].re("p c h -> p (c h)"), bq[:, 0:64], AF.Exp)
        act(sm["etot"][:, :, :].re("p c h -> p (c h)"), bq[:, 64:128], AF.Exp)
        tt("dve", sm["t16c"][:, :, :].re("p c h -> p (c h)"), bq[:, 64:128], sm["nacs"][:, :, :].re("p c h -> p (c h)"), ALU.add)
        act(sm["ds"][:, :, :], sm["t16c"][:, :, :], AF.Exp)
        tt("dve", sm["dtds"][:, :, :], sm["dt"][:, :, :], sm["ds"][:, :, :], ALU.mult)

        def proj(w, c, i_half, evac):
            b = bank()
            for kc in range(KD):
                mm(b[:, :], nTv(kc, slice(c * 128, (c + 1) * 128)), w[:, kc * 512:(kc + 1) * 512],
                   start=(kc == 0), stop=(kc == KD - 1))
            evac(b, c, i_half)

        def ev_u(b, c, ih):
            act(Gv(8 + 2 * c + ih), b[:, :], AF.Gelu)

        def ev_z(b, c, ih):
            act(zB[:, c, ih * 512:(ih + 1) * 512], b[:, :], AF.Silu)

        def ev_v(b, c, ih):
            act(v_g[c % 2][:, ih * 512:(ih + 1) * 512], b[:, :], AF.Gelu)
            o_, i_ = st6[:, ih, :], v_g[c % 2][:, ih * 512:(ih + 1) * 512]
            S.op("dve", lambda e: e.bn_stats(out=o_.ap, in_=i_.ap), outs=[o_], ins=[i_])

        for ih in range(2):
            w = bgR.take("uvz", ih)
            for c in range(NCHK):
                proj(w, c, ih, ev_u)
        wv0 = bgR.take("uvz", 2)
        wv1 = bgR.take("uvz", 3)
        for c in range(NCHK):
            proj(wv0, c, 0, ev_v)
            proj(wv1, c, 1, ev_v)
            o_, i_ = mv[:, :], st6[:, :, :].re("p a b -> p (a b)")
            S.op("dve", (lambda o_=o_, i_=i_: lambda e: e.bn_aggr(out=o_.ap, in_=i_.ap))(), outs=[o_], ins=[i_])
            act(s1["lnr"][:, 0:1], mv[:, 1:2], AF.Ln, bias=epsb[:, 0:1], scale=1.0)
            act(s1["lnr2"][:, 0:1], s1["lnr"][:, 0:1], AF.Exp, scale=-0.5)
            stt(s1["lnb"][:, 0:1], mv[:, 0:1], -1.0, s1["lnr2"][:, 0:1], ALU.mult, ALU.mult)
            for ih in range(2):
                act(Gv(16 + 2 * c + ih), v_g[c % 2][:, ih * 512:(ih + 1) * 512], AF.Identity,
                    bias=s1["lnb"][:, 0:1], scale=s1["lnr2"][:, 0:1])
        for ih in range(2):
            w = bgR.take("uvz", 4 + ih)
            for c in range(NCHK):
                proj(w, c, ih, ev_z)
        def stage_a(c):
            q = c % 2
            cols = slice(c * 128, (c + 1) * 128)
            bt2 = bank()
            bt2b = bt2[:, :].bitcast(BF16)
            for j in range(8):
                tr(bt2b[:, j * 128:(j + 1) * 128], xbcT[:, j, cols], identb[:, :])
            bt3 = bank()
            bt3b = bt3[:, :].bitcast(BF16)
            for g in range(2):
                tr(bt3b[:, g * 128:(g + 1) * 128], xbcT[:, 8 + g, cols], identb[:, :])
            bc_ = bank()
            for g in range(2):
                mm(bc_[:, g * 128:(g + 1) * 128], xbcT[:, 8 + g, cols], xbcT[:, 10 + g, cols])
            cp("act", xs_tok[q][:, :], bt2b[:, :])
            cp("act", Btok[q][:, :], bt3b[:, 0:256])
            for g in range(2):
                tt("dve", cbm[q][:, g, :], bc_[:, g * 128:(g + 1) * 128], mle[:, :], ALU.mult)
            xs3 = xs_tok[q][:, :].re("p (h q) -> p h q", h=NH)
            tt("dve", xdt[q][:, :].re("p (h q) -> p h q", h=NH), xs3,
               sm["dt"][:, c, :].re("p (h o) -> p h o", o=1).bcast([128, NH, HP]), ALU.mult)
            tt("pool", xdtds[q][:, :].re("p (h q) -> p h q", h=NH), xs3,
               sm["dtds"][:, c, :].re("p (h o) -> p h o", o=1).bcast([128, NH, HP]), ALU.mult)
            bks = []
            for hq in range(4):
                bk = bank()
                bks.append(bk)
                rb = Rb[hq % 2]
                tt("dve", rb[:, :, :], mle[:, :].re("p (o l) -> p o l", o=1).bcast([128, 4, 128]),
                   sm["adt"][:, c, 4 * hq:4 * hq + 4].re("p (h o) -> p h o", o=1).bcast([128, 4, 128]), ALU.mult)
                for j in range(4):
                    mm(bk[:, j * 128:(j + 1) * 128], onesf[:, :], rb[:, j, :], start=True, stop=False)
                    mm(bk[:, j * 128:(j + 1) * 128], identb[:, :], negm[:, :], start=False, stop=True)
            for hq in range(4):
                for j in range(4):
                    h = hq * 4 + j
                    act(decT[q][:, h, :], bks[hq][:, j * 128:(j + 1) * 128], AF.Exp, bias=sm["nacs"][:, c, h:h + 1], scale=1.0)
            for g in range(2):
                tt("dve", decT[q][:, 8 * g:8 * g + 8, :], decT[q][:, 8 * g:8 * g + 8, :],
                   cbm[q][:, g:g + 1, :].bcast([128, 8, 128]), ALU.mult)

        def stage_a2(c):
            q = c % 2
            cols = slice(c * 128, (c + 1) * 128)
            bs_ = [bank(), bank()]
            for h in range(8):
                vh = Gv(16 + 2 * c + h // 4, slice((h % 4) * 128, (h % 4 + 1) * 128))
                mm(bs_[h // 4][:, (h % 4) * 128:(h % 4 + 1) * 128], WsT[:, h, :], vh)
            for ih in range(2):
                fs = slice(ih * 512, (ih + 1) * 512)
                tt("dve", mx[q][:, fs], bs_[ih][:, :], glnb[:, fs], ALU.mult)
                tt("pool", mx[q][:, fs], mx[q][:, fs], CstT[:, fs], ALU.add)
                tt("dve", mx[q][:, fs], mx[q][:, fs], Gv(8 + 2 * c + ih), ALU.mult)
            act(yan[q][:, :], mx[q][:, :], AF.Square, accum=s1["ssa"][:, q:q + 1])
            act(s1["ra"][:, q:q + 1], s1["ssa"][:, q:q + 1], AF.Ln, bias=epsb[:, 0:1], scale=1.0 / 1024)
            act(s1["ra2"][:, q:q + 1], s1["ra"][:, q:q + 1], AF.Exp, scale=-0.5)
            act(yan[q][:, :], mx[q][:, :], AF.Copy, scale=s1["ra2"][:, q:q + 1])
            bt = bank()
            btb = bt[:, :].bitcast(BF16)
            for kc in range(8):
                tr(btb[:, kc * 128:(kc + 1) * 128], yan[q][:, kc * 128:(kc + 1) * 128], identb[:, :])
            cp("dve", nT.v((slice(None), slice(0, 8), cols), [("nT", k) for k in range(8)]),
               btb[:, :].re("p (j t) -> p j t", j=8))

        def stage_b(c):
            q = c % 2
            cols = slice(c * 128, (c + 1) * 128)
            by = [bank(), bank()]
            for h in range(NH):
                mm(by[h // 8][:, (h % 8) * 64:(h % 8 + 1) * 64], decT[q][:, h, :], xdt[q][:, h * 64:(h + 1) * 64])
            bo = [bank(), bank()]
            for g in range(2):
                mm(bo[g][:, :], xbcT[:, 10 + g, cols], Sbf[:, g * 512:(g + 1) * 512])
            bst = [bank(), bank()]
            for g in range(2):
                mm(bst[g][:, :], Btok[q][:, g * 128:(g + 1) * 128], xdtds[q][:, g * 512:(g + 1) * 512])
            for g in range(2):
                fs = slice(g * 512, (g + 1) * 512)
                tt("pool", Sst[:, fs].re("p (h q) -> p h q", h=8), Sst[:, fs].re("p (h q) -> p h q", h=8),
                   sm["etot"][:, c, 8 * g:8 * g + 8].re("p (h o) -> p h o", o=1).bcast([128, 8, HP]), ALU.mult)
            for g in range(2):
                fs = slice(g * 512, (g + 1) * 512)
                tt("dve", y1[q][:, fs].re("p (h q) -> p h q", h=8), bo[g][:, :].re("p (h q) -> p h q", h=8),
                   sm["eacs"][:, c, 8 * g:8 * g + 8].re("p (h o) -> p h o", o=1).bcast([128, 8, HP]), ALU.mult)
                tt("dve", Sst[:, fs], bst[g][:, :], Sst[:, fs], ALU.add)
                cp("act", Sbf[:, fs], Sst[:, fs])
            for g in range(2):
                fs = slice(g * 512, (g + 1) * 512)
                tt("dve", y1[q][:, fs], by[g][:, :], y1[q][:, fs], ALU.add)
                tt("pool", y2[:, fs].re("p (h q) -> p h q", h=8), xs_tok[q][:, fs].re("p (h q) -> p h q", h=8),
                   dsk[:, 8 * g:8 * g + 8].re("p (h o) -> p h o", o=1).bcast([128, 8, HP]), ALU.mult)
                tt("dve", y2[:, fs], y2[:, fs], y1[q][:, fs], ALU.add)
                tt("dve", y2[:, fs], y2[:, fs], zB[:, c, fs], ALU.mult)
                act(ybn[q][:, fs], y2[:, fs], AF.Square, accum=s1["ssb"][:, 2 * q + g:2 * q + g + 1])

        def stage_b2(c):
            q = c % 2
            cols = slice(c * 128, (c + 1) * 128)
            act(s1["rb"][:, 2 * q:2 * q + 2], s1["ssb"][:, 2 * q:2 * q + 2], AF.Ln, bias=epsb[:, 0:1], scale=1.0 / 512)
            act(s1["rb2"][:, 2 * q:2 * q + 2], s1["rb"][:, 2 * q:2 * q + 2], AF.Exp, scale=-0.5)
            for g in range(2):
                fs = slice(g * 512, (g + 1) * 512)
                act(ybn[q][:, fs], y2[:, fs], AF.Copy, scale=s1["rb2"][:, 2 * q + g:2 * q + g + 1])
            bt4 = bank()
            bt4b = bt4[:, :].bitcast(BF16)
            for kc in range(8):
                tr(bt4b[:, kc * 128:(kc + 1) * 128], ybn[q][:, kc * 128:(kc + 1) * 128], identb[:, :])
            cp("act", G.v((slice(None), slice(0, 8), cols), [("G", k) for k in range(8)]),
               bt4b[:, :].re("p (j t) -> p j t", j=8))

        stage_a(0)
        stage_a2(0)
        for c in range(NCHK):
            if c + 1 < NCHK:
                stage_a(c + 1)
            stage_b(c)
            if c + 1 < NCHK:
                stage_a2(c + 1)
            stage_b2(c)
        for dc in range(KD):
            w = bgR.take("wo", dc)
            b = bank()
            for kc in range(16):
                rhs = nTv(kc) if kc < 8 else Gv(kc - 8)
                mm(b[:, :], w[:, kc * 128:(kc + 1) * 128], rhs, start=(kc == 0), stop=(kc == 15))
            tt("dve", hTv(dc), b[:, :], hTv(dc), ALU.add)


    def ple(n):
        load_p(n)
        norm_to_nT(3)
        wp = None
        for dc in range(KD):
            w = smR.take("pg", dc)
            wp = smR.take("pp", dc)
            b = bank()
            for kc in range(KD):
                mm(b[:, :], w[:, kc * 128:(kc + 1) * 128], nTv(kc), start=(kc == 0), stop=(kc == KD - 1))
            g_ = cacc[dc % 2]
            act(g_[:, :], b[:, :], AF.Sigmoid, bias=vecT[:, 5, dc:dc + 1], scale=1.0)
            b2 = bank()
            for kc in range(2):
                mm(b2[:, :], wp[:, kc * 128:(kc + 1) * 128], pT[:, kc, :], start=(kc == 0), stop=(kc == 1))
            tt("dve", g_[:, :], b2[:, :], g_[:, :], ALU.mult)
            tt("pool", hTv(dc), hTv(dc), g_[:, :], ALU.add)

    def final_and_store(n, do_norm=True):
        if do_norm:
            rmsnorm_T(4)
            for kc in range(KD):
                stt(hTv(kc), hTv(kc), vecT[:, 4, kc:kc + 1], rstd[:, :], ALU.mult, ALU.mult)
        for c in range(NCHK):
            cols = slice(c * 128, (c + 1) * 128)
            ot = io[c % 2]
            for half in range(2):
                b = bank()
                for j in range(4):
                    kc = half * 4 + j
                    tr(b[:, j * 128:(j + 1) * 128], hTv(kc, cols), identf[:, :])
                cp("act" if half == 0 else "dve", ot[:, half * 512:(half + 1) * 512], b[:, :])
            r0 = n * T + c * 128
            S.dma("sp", f"io{c % 2}", dv(out_d[r0:r0 + 128, :], ("out", n, c)), ot[:, :])

    conv_late()
    for n in range(nt):
        load_x(n)
        if dbg == "x":
            final_and_store(n, False); continue
        ffn(1, 0)
        if dbg == "ffn1":
            final_and_store(n, False); continue
        if n == 0:
            S.dma("sp", "wdt", wdt[:, :], dv(sc["dt"], ("dt", 0)))
        mixer(n)
        if dbg == "mix":
            final_and_store(n, False); continue
        ffn(2, 2)
        if dbg == "ffn2":
            final_and_store(n, False); continue
        ple(n)
        if dbg == "ple":
            final_and_store(n, False); continue
        final_and_store(n)
    S.final_wait_all("sp")

    sems = {e: es.enter_context(nc.semaphore("s_" + e)) for e in ENGS}
    dma_sems = {k: es.enter_context(nc.semaphore("d_" + k)) for k in S.dma_cnt}
    block = es.enter_context(nc.Block())
    S.emit_all(nc, block, sems, dma_sems)
    es.close()
    return nc


_NC_CACHE = {}


def kernel(**inputs):
    nt = SEQ // T
    if nt not in _NC_CACHE:
        _NC_CACHE[nt] = build_nc(nt)
    nc = _NC_CACHE[nt]
    x = np.ascontiguousarray(inputs["x"], dtype=np.float32)
    p = np.ascontiguousarray(inputs["p"], dtype=np.float32)
    wmap = {n: np.ascontiguousarray(inputs[n], dtype=np.float32) for n in WNAMES}
    in_maps = []
    for c in range(N_CORES):
        m = {"x": x[c], "p": p[0, c]}
        m.update(wmap)
        in_maps.append(m)
    res = run_bass_kernel_spmd(nc, in_maps, core_ids=list(range(N_CORES)))
    return np.stack([np.asarray(r["out"], dtype=np.float32) for r in res.results], axis=0)
```
